# Optimizing a Trainium2 kernel written in Bass

```python
import jax, jax.numpy as jnp
from jax import lax
import numpy as np

D_MODEL = 1024
BATCH = 2
SEQ = 8192
DEPTH = 1

GRID_W = 64
CTX_LEN = 256
D_MIX = 2 * D_MODEL
D_POOL = D_MIX // 2
D_NA = D_MIX - D_POOL
POOL_WINDOWS = (2, 4, 8, 16)
N_POOL_GROUPS = len(POOL_WINDOWS)
POOL_GROUP_DIM = D_POOL // N_POOL_GROUPS
HEAD_DIM = 64
N_HEADS = D_NA // HEAD_DIM
ROPE_AXIS_DIM = HEAD_DIM // 2
NA_ROWS_MAX = 8
NA_COLS = 16
QUERY_BLOCK = 128
ROPE_BASE = 10000.0
LN_EPS = 1e-6
DEEPNORM_ALPHA = (2.0 * DEPTH) ** 0.25
DEEPNORM_BETA = (8.0 * DEPTH) ** -0.25
D_IN = 2 * D_POOL + 4 * D_NA
SPLIT_POINTS = [D_POOL, 2 * D_POOL, 2 * D_POOL + D_NA, 2 * D_POOL + 2 * D_NA, 2 * D_POOL + 3 * D_NA]

kernel_name = "hybrid_pool_natten_dit_layer"


def layer_norm(x):
    x32 = x.astype(jnp.float32)
    mu = jnp.mean(x32, axis=-1, keepdims=True)
    var = jnp.mean(jnp.square(x32 - mu), axis=-1, keepdims=True)
    return ((x32 - mu) * lax.rsqrt(var + LN_EPS)).astype(x.dtype)


def ada_mod(cvec, w_ada, b_ada):
    return jnp.split(jax.nn.silu(cvec) @ w_ada + b_ada, 3, axis=-1)


def modulate(x, shift, scale):
    return layer_norm(x) * (1 + scale) + shift


def split_heads(a):
    return a.reshape(a.shape[:-1] + (N_HEADS, HEAD_DIM))


def rope_axis(x, pos):
    half = x.shape[-1] // 2
    inv_freq = ROPE_BASE ** (-jnp.arange(half, dtype=jnp.float32) / half)
    ang = pos.astype(jnp.float32)[:, None] * inv_freq
    cos = jnp.cos(ang)[:, None, :]
    sin = jnp.sin(ang)[:, None, :]
    x1, x2 = x[..., :half], x[..., half:]
    return jnp.concatenate([x1 * cos - x2 * sin, x1 * sin + x2 * cos], axis=-1).astype(x.dtype)


def rope_2d(x, row, col):
    return jnp.concatenate([rope_axis(x[..., :ROPE_AXIS_DIM], row),
                            rope_axis(x[..., ROPE_AXIS_DIM:], col)], axis=-1)


def multiscale_pool(u, w_pool, pool_scale):
    b, L, _ = u.shape
    ug = u.reshape(b, L, N_POOL_GROUPS, POOL_GROUP_DIM)
    cs = jnp.concatenate([jnp.zeros((b, 1, N_POOL_GROUPS, POOL_GROUP_DIM), jnp.float32),
                          jnp.cumsum(ug.astype(jnp.float32), axis=1)], axis=1)
    t = jnp.arange(L)
    means = []
    for g, w in enumerate(POOL_WINDOWS):
        lo = jnp.clip(t - w // 2, 0, L)
        hi = jnp.clip(t - w // 2 + w, 0, L)
        cs_g = cs[:, :, g]
        means.append((cs_g[:, hi] - cs_g[:, lo]) / (hi - lo).astype(jnp.float32)[None, :, None])
    pooled = jnp.stack(means, axis=2).astype(u.dtype) - ug
    y = jnp.einsum('blgc,gcd->blgd', pooled, w_pool)
    return y.reshape(b, L, D_POOL) * pool_scale


def neighbourhood_indices(L):
    rows = L // GRID_W
    kr = min(NA_ROWS_MAX, rows)
    t = jnp.arange(L)
    row, col = t // GRID_W, t % GRID_W
    rs = jnp.clip(row - kr // 2, 0, rows - kr)
    cs = jnp.clip(col - NA_COLS // 2, 0, GRID_W - NA_COLS)
    key_r = rs[:, None] + jnp.arange(kr)
    key_c = cs[:, None] + jnp.arange(NA_COLS)
    idx = (key_r[:, :, None] * GRID_W + key_c[:, None, :]).reshape(L, kr * NA_COLS)
    dr = jnp.broadcast_to((key_r - row[:, None] + NA_ROWS_MAX - 1)[:, :, None],
                          (L, kr, NA_COLS)).reshape(L, kr * NA_COLS)
    dc = jnp.broadcast_to((key_c - col[:, None] + NA_COLS - 1)[:, None, :],
                          (L, kr, NA_COLS)).reshape(L, kr * NA_COLS)
    return row, col, idx, dr, dc


def neighbourhood_attention(q, k, v, k_ctx, v_ctx, rpb, row, col, idx, dr, dc):
    b, L, h, dh = q.shape
    scale = dh ** -0.5
    q_rot = rope_2d(q, row, col)
    k_rot = rope_2d(k, row, col)
    nb = L // QUERY_BLOCK
    to_blocks = lambda a: jnp.moveaxis(a.reshape((b, nb, QUERY_BLOCK) + a.shape[2:]), 1, 0)
    idx_blocks = lambda a: a.reshape((nb, QUERY_BLOCK) + a.shape[1:])

    def one_block(args):
        qr, qp, ib, drb, dcb = args
        kg = k_rot[:, ib]
        vg = v[:, ib]
        s_loc = (jnp.einsum('bqhd,bqkhd->bhqk', qr, kg).astype(jnp.float32) * scale
                 + rpb[:, drb, dcb][None].astype(jnp.float32))
        s_ctx = jnp.einsum('bqhd,bchd->bhqc', qp, k_ctx).astype(jnp.float32) * scale
        p = jax.nn.softmax(jnp.concatenate([s_loc, s_ctx], axis=-1), axis=-1).astype(v.dtype)
        n_loc = ib.shape[1]
        return (jnp.einsum('bhqk,bqkhd->bqhd', p[..., :n_loc], vg)
                + jnp.einsum('bhqc,bchd->bqhd', p[..., n_loc:], v_ctx))

    out = lax.map(one_block, (to_blocks(q_rot), to_blocks(q), idx_blocks(idx),
                              idx_blocks(dr), idx_blocks(dc)))
    return jnp.moveaxis(out, 0, 1).reshape(b, L, h * dh)


def context_attention(q, k, v):
    b, C, h, dh = q.shape
    s = jnp.einsum('bqhd,bkhd->bhqk', q, k).astype(jnp.float32) * dh ** -0.5
    p = jax.nn.softmax(s, axis=-1).astype(v.dtype)
    return jnp.einsum('bhqk,bkhd->bqhd', p, v).reshape(b, C, h * dh)


def setup_inputs(seed: int = 0) -> dict:
    key = jax.random.key(seed)
    ks = jax.random.split(key, 15)
    f32 = jnp.float32
    nrm = lambda k, shape: jax.random.normal(k, shape, f32)
    col_scale = jnp.concatenate([
        jnp.full((D_POOL,), DEEPNORM_BETA, f32),
        jnp.ones((D_POOL,), f32),
        jnp.ones((2 * D_NA,), f32),
        jnp.full((D_NA,), DEEPNORM_BETA, f32),
        jnp.ones((D_NA,), f32)])
    return {
        "x": nrm(ks[0], (BATCH, SEQ, D_MODEL)),
        "c": nrm(ks[1], (BATCH, D_MODEL)),
        "ctx": nrm(ks[2], (BATCH, CTX_LEN, D_MODEL)),
        "c_ctx": nrm(ks[3], (D_MODEL,)),
        "w_ada": nrm(ks[4], (DEPTH, D_MODEL, 3 * D_MODEL)) * (0.5 * D_MODEL ** -0.5),
        "b_ada": 0.01 * nrm(ks[5], (DEPTH, 3 * D_MODEL)),
        "w_in": nrm(ks[6], (DEPTH, D_MODEL, D_IN)) * (D_MODEL ** -0.5) * col_scale,
        "b_in": 0.01 * nrm(ks[7], (DEPTH, D_IN)),
        "w_pool": nrm(ks[8], (DEPTH, N_POOL_GROUPS, POOL_GROUP_DIM, POOL_GROUP_DIM)) * POOL_GROUP_DIM ** -0.5,
        "pool_scale": 1.0 + 0.1 * nrm(ks[9], (DEPTH, D_POOL)),
        "rpb": 0.1 * nrm(ks[10], (DEPTH, N_HEADS, 2 * NA_ROWS_MAX - 1, 2 * NA_COLS - 1)),
        "w_out": nrm(ks[11], (DEPTH, D_MIX, D_MODEL)) * (D_MIX ** -0.5) * DEEPNORM_BETA,
        "b_out": 0.01 * nrm(ks[12], (DEPTH, D_MODEL)),
        "ln_g": 1.0 + 0.05 * nrm(ks[13], (DEPTH, D_MODEL)),
        "ln_b": 0.01 * nrm(ks[14], (DEPTH, D_MODEL)),
    }


def reference(x, c, ctx, c_ctx, w_ada, b_ada, w_in, b_in, w_pool, pool_scale, rpb,
              w_out, b_out, ln_g, ln_b):
    L = x.shape[1]
    row, col, idx, dr, dc = neighbourhood_indices(L)
    ctx_s = ctx
    for i in range(DEPTH):
        last = i == DEPTH - 1
        sh_x, sc_x, g_x = ada_mod(c, w_ada[i], b_ada[i])
        sh_c, sc_c, g_c = ada_mod(c_ctx, w_ada[i], b_ada[i])
        hx = modulate(x, sh_x[:, None], sc_x[:, None])
        hc = modulate(ctx_s, sh_c, sc_c)
        ux, zpx, qx, kx, vx, zax = jnp.split(hx @ w_in[i] + b_in[i], SPLIT_POINTS, axis=-1)
        uc, zpc, qc, kc, vc, zac = jnp.split(hc @ w_in[i] + b_in[i], SPLIT_POINTS, axis=-1)
        k_ctx, v_ctx = split_heads(kc), split_heads(vc)
        pool_x = multiscale_pool(ux, w_pool[i], pool_scale[i]) * jax.nn.silu(zpx)
        na_x = neighbourhood_attention(split_heads(qx), split_heads(kx), split_heads(vx),
                                       k_ctx, v_ctx, rpb[i], row, col, idx, dr, dc) * jax.nn.silu(zax)
        y_x = jnp.concatenate([pool_x, na_x], axis=-1) @ w_out[i] + b_out[i]
        x_next = layer_norm(DEEPNORM_ALPHA * x + g_x[:, None] * y_x) * ln_g[i] + ln_b[i]
        if not last:
            pool_c = multiscale_pool(uc, w_pool[i], pool_scale[i]) * jax.nn.silu(zpc)
            na_c = context_attention(split_heads(qc), k_ctx, v_ctx) * jax.nn.silu(zac)
            y_c = jnp.concatenate([pool_c, na_c], axis=-1) @ w_out[i] + b_out[i]
            ctx_s = layer_norm(DEEPNORM_ALPHA * ctx_s + g_c * y_c) * ln_g[i] + ln_b[i]
        x = x_next
    return x
```

```python
import numpy as np
import ml_dtypes
import concourse.bass as bass
import concourse.mybir as mybir
from concourse.bass_utils import run_bass_kernel_spmd

F32 = mybir.dt.float32
BF16 = mybir.dt.bfloat16
U8 = mybir.dt.uint8
ALU = mybir.AluOpType
AF = mybir.ActivationFunctionType

D = 1024
L = 8192
GW = 64
NSLOT = 2560
NOWN = 2048
OWN0 = 256
NCORES = 8
ALPHA = float((2.0 * 1) ** 0.25)
LN_EPS = 1e-6
NEG = -30000.0
NTAB = 11
SPECIAL = {(0, -2): 5, (0, -1): 6, (1, -2): 7, (14, 2): 8, (15, 1): 9, (15, 2): 10}

DEBUG_OUT = None


class Op:
    __slots__ = ("eng", "fn", "deps", "chan", "dmaval", "sig", "sigval", "name")

    def __init__(self, eng, fn, chan, name):
        self.eng = eng
        self.fn = fn
        self.deps = set()
        self.chan = chan
        self.dmaval = 0
        self.sig = False
        self.sigval = 0
        self.name = name


class Sched:
    ENGS = ("pe", "act", "dve", "pool", "sp")

    def __init__(self):
        self.prog = {e: [] for e in self.ENGS}
        self.last_w = {}
        self.readers = {}
        self.chan_cnt = {}
        self.pending = {e: set() for e in self.ENGS}
        self.dma_ops = []

    def op(self, eng, fn, reads=(), writes=(), chan=None, name=""):
        o = Op(eng, fn, chan, name)
        deps = set()
        writes = list(writes) + [r for r in reads if isinstance(r, tuple) and r[0] == "psum"]
        reads = [r for r in reads if not (isinstance(r, tuple) and r[0] == "psum")]
        for r in reads:
            lw = self.last_w.get(r)
            if lw is not None:
                deps.add(lw)
        for w in writes:
            lw = self.last_w.get(w)
            if lw is not None:
                deps.add(lw)
            rd = self.readers.get(w)
            if rd:
                deps.update(rd.values())
        deps |= self.pending[eng]
        self.pending[eng] = set()
        if eng == "pe":
            deps = {d for d in deps if not (d.eng == "pe" and d.chan is None)}
        o.deps = deps
        for w in writes:
            self.last_w[w] = o
            self.readers[w] = {}
        for r in reads:
            key = eng if chan is None else ("dma", len(self.dma_ops))
            self.readers.setdefault(r, {})[key] = o
        if chan is not None:
            self.chan_cnt[chan] = self.chan_cnt.get(chan, 0) + 16
            o.dmaval = self.chan_cnt[chan]
            self.dma_ops.append(o)
        self.prog[eng].append(o)
        return o

    def barrier(self):
        lasts = set()
        for e in self.ENGS:
            for o in reversed(self.prog[e]):
                if o.chan is None:
                    lasts.add(o)
                    break
        latest = {}
        for o in self.dma_ops:
            latest[o.chan] = o
        lasts |= set(latest.values())
        for e in self.ENGS:
            self.pending[e] |= lasts

    def emit(self, nc):
        engobj = {"pe": nc.tensor, "act": nc.scalar, "dve": nc.vector, "pool": nc.gpsimd, "sp": nc.sync}
        for e in self.ENGS:
            for o in self.prog[e]:
                for d in o.deps:
                    if d.chan is None:
                        d.sig = True
        for e in self.ENGS:
            n = 0
            for o in self.prog[e]:
                if o.chan is None and o.sig:
                    n += 1
                    o.sigval = n
        esem = {e: nc.alloc_semaphore("es_" + e) for e in ("pe", "act", "dve", "pool")}
        csem = {c: nc.alloc_semaphore("cs_%d" % i) for i, c in enumerate(sorted(self.chan_cnt))}
        stats = {"waits": 0, "sigs": 0, "ops": 0}
        with nc.Block() as block:
            def run(e):
                def body(eng):
                    waited = {}
                    for o in self.prog[e]:
                        need = {}
                        for d in o.deps:
                            if d.chan is not None:
                                k = ("c", d.chan)
                                v = d.dmaval
                            else:
                                k = ("e", d.eng)
                                v = d.sigval
                            if v > need.get(k, 0):
                                need[k] = v
                        for k, v in need.items():
                            if v > waited.get(k, 0):
                                waited[k] = v
                                sem = csem[k[1]] if k[0] == "c" else esem[k[1]]
                                eng.wait_ge(sem, v)
                                stats["waits"] += 1
                        ins = o.fn(eng)
                        stats["ops"] += 1
                        if o.chan is not None:
                            ins.then_inc(csem[o.chan], 16)
                        elif o.sig:
                            ins.then_inc(esem[e], 1)
                            stats["sigs"] += 1
                    if e == "sp":
                        for c, v in self.chan_cnt.items():
                            eng.wait_ge(csem[c], v)
                return body
            block.tensor(run("pe"))
            block.scalar(run("act"))
            block.vector(run("dve"))
            block.gpsimd(run("pool"))
            block.sync(run("sp"))
        return stats


def _slot_rows(r0):
    g = np.arange(40) + r0 - 4
    g = np.where(g < 0, g + 8, g)
    g = np.where(g > 127, g - 8, g)
    return g


def _rope_tables(r0):
    srow = _slot_rows(r0)
    s = np.arange(NSLOT)
    grow = srow[s // GW].astype(np.float32)
    gcol = (s % GW).astype(np.float32)
    p = np.arange(128)
    d = p % 64
    isrow = d < 32
    dd = np.where(isrow, d, d - 32)
    i = dd % 16
    inv = (np.float32(10000.0) ** (-(i.astype(np.float32)) / np.float32(16.0))).astype(np.float32)
    pos = np.where(isrow[:, None], grow[None, :], gcol[None, :]).astype(np.float32)
    ang = (pos * inv[:, None]).astype(np.float32)
    C = np.cos(ang).astype(np.float32)
    Sn = np.sin(ang).astype(np.float32)
    S = np.where((dd < 16)[:, None], -Sn, Sn).astype(np.float32)
    return np.ascontiguousarray(C), np.ascontiguousarray(S)


def _perm_matrix():
    p = np.arange(128)
    partner = np.where((p % 32) < 16, p + 16, p - 16)
    m = np.zeros((128, 128), np.float32)
    m[partner, p] = 1.0
    return m


def _bias_tables(rpb, r0):
    srow = _slot_rows(r0)
    combos = [(8, o) for o in (2, 1, 0, -1, -2)] + sorted(SPECIAL, key=lambda k: SPECIAL[k])
    kp = np.arange(128)
    a = kp // 64
    kc = kp % 64
    qq = np.arange(128)
    b2 = qq // 64
    qc = qq % 64
    cs = np.clip(qc - 8, 0, GW - 16)
    colvalid = (kc[:, None] >= cs[None, :]) & (kc[:, None] < cs[None, :] + 16)
    dc = kc[:, None] - qc[None, :] + 15
    out = np.empty((16, 128, NTAB * 128), np.float32)
    for ti, (j, o) in enumerate(combos):
        kt = j + 2 + o
        sr = 2 * kt + a
        lr = 2 * j + b2
        rowvalid = (sr[:, None] >= lr[None, :]) & (sr[:, None] <= lr[None, :] + 7)
        gk = srow[sr]
        gq = r0 + lr
        dr = gk[:, None] - gq[None, :] + 7
        valid = rowvalid & colvalid
        drc = np.clip(dr, 0, 14)
        dcc = np.clip(dc, 0, 30)
        g = rpb[:, drc, dcc]
        out[:, :, ti * 128:(ti + 1) * 128] = np.where(valid[None], g, np.float32(NEG))
    out = out.reshape(8, 2, 128, NTAB * 128).transpose(0, 2, 1, 3)
    return np.ascontiguousarray(out)


def _pool_tables(r0):
    T0 = r0 * GW
    invc = np.empty((128, 4, 2, 16), np.float32)
    for g, w in enumerate((2, 4, 8, 16)):
        for side, base in enumerate((0, NOWN - 16)):
            tg = T0 + base + np.arange(16)
            lo = np.clip(tg - w // 2, 0, L)
            hi = np.clip(tg - w // 2 + w, 0, L)
            invc[:, g, side, :] = (np.float32(1.0) / (hi - lo).astype(np.float32))[None, :]
    um = np.empty((128, 2, 16), np.float32)
    um[:, 0, :] = ((T0 - 16 + np.arange(16)) >= 0).astype(np.float32)[None, :]
    um[:, 1, :] = ((T0 + NOWN + np.arange(16)) < L).astype(np.float32)[None, :]
    return invc, um


def _tvec(v, n):
    return np.ascontiguousarray(np.asarray(v, np.float32).reshape(n, 128).T)


def build_program(debug=None, stop=None, npairs=8, npool=4):
    nc = bass.Bass("TRN2", target_bir_lowering=False)
    S = Sched()

    def din(name, shape):
        return nc.dram_tensor(name, list(shape), F32, kind="ExternalInput").ap()

    xs = din("xs", [NSLOT, D])
    ctx = din("ctx", [256, D])
    cvec = din("cvec", [128, 16])
    w_ada = din("w_ada", [D, 3 * D])
    b_ada_t = din("b_ada_t", [128, 24])
    b_ada_g = din("b_ada_g", [1, D])
    w_in = din("w_in", [D, 6 * D])
    b_in_t = din("b_in_t", [128, 48])
    w_pool = din("w_pool", [4 * 256, 256])
    pscale_t = din("pscale_t", [128, 8])
    w_out = din("w_out", [2 * D, D])
    b_out = din("b_out", [1, D])
    ln_g = din("ln_g", [1, D])
    ln_b = din("ln_b", [1, D])
    ropeC = din("ropeC", [128, NSLOT])
    ropeS = din("ropeS", [128, NSLOT])
    perm_d = din("perm", [128, 128])
    ident_d = din("ident", [128, 128])
    btab = din("btab", [8, 128, 2 * NTAB * 128])
    invc_d = din("invc", [128, 128])
    umask_d = din("umask", [128, 32])
    out = nc.dram_tensor("out", [NOWN, D], F32, kind="ExternalOutput").ap()
    rd_dram = nc.dram_tensor("rd_scratch", [64, 512], F32).ap()
    dbg = {}
    if debug:
        for k, (shp, dt_) in debug.items():
            dbg[k] = nc.dram_tensor("dbg_" + k, list(shp), dt_, kind="ExternalOutput").ap()

    ARENA = 204 * 1024
    arena = nc.alloc_sbuf_tensor("arena", [128, ARENA], U8)
    cur = [0]

    def alloc(nbytes):
        off = cur[0]
        nb = (nbytes + 63) // 64 * 64
        cur[0] += nb
        assert cur[0] <= ARENA, ("SBUF arena overflow", cur[0])
        return off

    def view(off, shape, dt):
        esz = 4 if dt == F32 else 2
        n = int(np.prod(shape))
        ap = arena[:, off:off + n * esz].bitcast(dt)
        if len(shape) == 2:
            return ap.rearrange("p (a b) -> p a b", b=shape[1])
        if len(shape) == 3:
            return ap.rearrange("p (a b c) -> p a b c", b=shape[1], c=shape[2])
        return ap

    def tile(shape, dt):
        esz = 4 if dt == F32 else 2
        return view(alloc(int(np.prod(shape)) * esz), shape, dt)

    ident = tile([128], BF16)
    perm = tile([128], BF16)
    ones_bf = tile([128], BF16)
    selB = tile([128], BF16)
    cs_f = tile([16], F32)
    cs_b = tile([16], BF16)
    ada = tile([32], F32)
    b_ada_sb = tile([24], F32)
    b_in_sb = tile([48], F32)
    bq8 = tile([8], F32)
    pscale = tile([8], F32)
    invc = tile([128], F32)
    umask = tile([32], F32)
    stt = tile([3 * 12], F32)
    mv = tile([3 * 2], F32)
    rs = tile([3], F32)
    nmr = tile([3], F32)
    hx_off = (cur[0] + 63) // 64 * 64
    hxT = tile([8, NSLOT], BF16)
    hcT = tile([8, 256], BF16)
    naT = tile([8, NOWN], BF16)
    g_bc = tile([D], F32)
    persist_end = cur[0]

    psall_t = nc.alloc_psum_tensor("psall", [128, 4096], F32)
    psall = psall_t[:, :]

    def bk(i):
        return psall[:, i * 512:(i + 1) * 512]

    def bkbf(i):
        return psall[:, i * 512:(i + 1) * 512].bitcast(BF16)

    PB = lambda i: ("psum", i)

    def dma(eng, out_ap, in_ap, chan, reads=(), writes=(), name="dma"):
        return S.op(eng, lambda e: e.dma_start(out=out_ap, in_=in_ap), reads, writes, chan=chan, name=name)

    def mm(out_ap, lhsT, rhs, start, stop, reads, writes, name="mm"):
        return S.op("pe", lambda e: e.matmul(out_ap, lhsT, rhs, start=start, stop=stop), reads, writes, name=name)

    def act(out_ap, in_ap, func, bias=None, scale=None, reads=(), writes=(), name="act"):
        kw = {}
        if bias is not None:
            kw["bias"] = bias
        if scale is not None:
            kw["scale"] = scale
        return S.op("act", lambda e: e.activation(out_ap, in_ap, func, **kw), reads, writes, name=name)

    def tt(eng, out_ap, a, b, op, reads=(), writes=(), name="tt"):
        return S.op(eng, lambda e: e.tensor_tensor(out_ap, a, b, op), reads, writes, name=name)

    def ts(eng, out_ap, a, s1, s2, op0, op1=None, reads=(), writes=(), name="ts"):
        if op1 is None:
            return S.op(eng, lambda e: e.tensor_scalar(out_ap, a, s1, None, op0), reads, writes, name=name)
        return S.op(eng, lambda e: e.tensor_scalar(out_ap, a, s1, s2, op0, op1), reads, writes, name=name)

    def stt_op(eng, out_ap, a, scalar, b, op0, op1, reads=(), writes=(), name="stt"):
        return S.op(eng, lambda e: e.scalar_tensor_tensor(out_ap, a, scalar, b, op0, op1), reads, writes, name=name)

    def cp(eng, out_ap, in_ap, reads=(), writes=(), name="cp"):
        return S.op(eng, lambda e: e.tensor_copy(out_ap, in_ap), reads, writes, name=name)

    def mset(eng, ap, val, writes=(), name="memset"):
        return S.op(eng, lambda e: e.memset(ap, val), (), writes, name=name)

    def dbg_dump(key, ap, reads):
        if key in dbg:
            dma("sp", dbg[key], ap, "dbg_" + key, reads=reads, name="dbg")

    def finish():
        stats = S.emit(nc)
        return nc, stats

    dma("pool", ident, ident_d, "c_ident", writes=["ident"])
    dma("pool", perm, perm_d, "c_perm", writes=["perm"])
    dma("sp", cs_f, cvec, "c_cvec", writes=["cs_f"])
    dma("sp", b_ada_sb, b_ada_t, "c_bada", writes=["b_ada_sb"])
    dma("sp", b_in_sb, b_in_t, "c_bin", writes=["b_in_sb"])
    dma("sp", pscale, pscale_t, "c_psc", writes=["pscale"])
    dma("sp", invc, invc_d, "c_invc", writes=["invc"])
    dma("sp", umask, umask_d, "c_umask", writes=["umask"])
    mset("dve", ones_bf, 1.0, writes=["ones_bf"])
    mset("dve", selB[0:1, 0:64], 0.0, writes=["selB"])
    mset("dve", selB[0:1, 64:128], 1.0, writes=["selB"])
    ts("dve", bq8, b_in_sb[:, 16:24], 0.125, None, ALU.mult, reads=["b_in_sb"], writes=["bq8"])
    act(cs_f, cs_f, AF.Silu, reads=["cs_f"], writes=["cs_f"], name="silu_c")
    cp("dve", cs_b, cs_f, reads=["cs_f"], writes=["cs_b"])

    ph0 = cur[0]
    wada = tile([8, 3 * D], BF16)
    csrep = tile([8, 128], BF16)
    gtmp = tile([D], F32)
    w_ada_v = w_ada.rearrange("(kc p) n -> p kc n", p=128)
    for q in range(4):
        dma("pool", wada[:, 2 * q:2 * q + 2, 0:2 * D], w_ada_v[:, 2 * q:2 * q + 2, 0:2 * D], "wada%d" % q,
            writes=[("wada", q)])
    dma("sp", gtmp, b_ada_g.partition_broadcast(128), "c_gtmp", writes=["gtmp"])
    cs_b3 = cs_b.rearrange("p (k j) -> p k j", j=2)
    cs_f3 = cs_f.rearrange("p (k j) -> p k j", j=2)
    g_bc_off = None

    def sc1(kc, j):
        return ada3[:, 8 + kc, j:j + 1]

    def sh(kc, j):
        return ada3[:, kc, j:j + 1]

    xt = [tile([D], F32) for _ in range(4)]
    xn = [[tile([D], BF16) for _ in range(4)] for _ in range(2)]
    ada3 = ada.rearrange("p (o j) -> p o j", j=2)

    def emit_ada():
        for oc in range(16):
            for kc in range(8):
                mm(bk(4)[:, 2 * oc:2 * oc + 2], wada[:, kc, oc * 128:(oc + 1) * 128], cs_b3[:, kc, :],
                   kc == 0, kc == 7, reads=[("wada", kc // 2), "cs_b"], writes=[PB(4)], name="ada_mm")
        ps_ada = bk(4)[:, 0:32].rearrange("p (o j) -> p o j", j=2)
        for j in range(2):
            tt("dve", ada3[:, :, j], ps_ada[:, :, j], b_ada_sb[:, 0:16], ALU.add,
               reads=[PB(4), "b_ada_sb"], writes=["ada"])
        ts("dve", ada3[:, 8:16, :], ada3[:, 8:16, :], 1.0, None, ALU.add, reads=["ada"], writes=["ada"])
        for kc in range(8):
            ts("dve", csrep[:, kc, :], ones_bf, cs_f3[:, kc, 0:1], None, ALU.mult,
               reads=["ones_bf", "cs_f"], writes=["csrep"])

    def emit_g_load():
        for q in range(4):
            dma("pool", wada[:, 2 * q:2 * q + 2, 2 * D:3 * D], w_ada_v[:, 2 * q:2 * q + 2, 2 * D:3 * D], "wadag%d" % q,
                writes=[("wadag", q)])

    def emit_g():
        for n in range(2):
            for kc in range(8):
                mm(bk(6 + n), csrep[:, kc, :], wada[:, kc, 2 * D + n * 512:2 * D + (n + 1) * 512],
                   kc == 0, kc == 7, reads=["csrep", ("wadag", kc // 2)], writes=[PB(6 + n)], name="g_mm")
            tt("dve", g_bc[:, n * 512:(n + 1) * 512], bk(6 + n), gtmp[:, n * 512:(n + 1) * 512], ALU.add,
               reads=[PB(6 + n), "gtmp"], writes=["g_bc"])


    if stop == "0":
        emit_ada()
        emit_g()
        dbg_dump("ada", ada, ["ada"])
        dbg_dump("g_bc", g_bc, ["g_bc"])
        return finish()

    tiles = [(xs[(g_ * 4 + t_) * 128:(g_ * 4 + t_ + 1) * 128, :], g_, t_) for g_ in range(5) for t_ in range(4)]
    tiles += [(ctx[t_ * 128:(t_ + 1) * 128, :], 5, t_) for t_ in range(2)]

    def ln_part1(i):
        src_ap, grp, t4 = tiles[i]
        xb = i % 4
        sb = i % 3
        dma("sp", xt[xb], src_ap, "xt%d" % xb, writes=[("xt", xb)], name="ld_x")
        for hf in range(2):
            S.op("dve", lambda e, hf=hf: e.bn_stats(stt[:, sb * 12 + hf * 6: sb * 12 + hf * 6 + 6],
                                                    xt[xb][:, hf * 512:(hf + 1) * 512]),
                 reads=[("xt", xb)], writes=[("stt", sb)], name="bn_stats")
        S.op("dve", lambda e: e.bn_aggr(mv[:, sb * 2: sb * 2 + 2], stt[:, sb * 12: sb * 12 + 12]),
             reads=[("stt", sb)], writes=[("mv", sb)], name="bn_aggr")
        act(rs[:, sb:sb + 1], mv[:, sb * 2 + 1: sb * 2 + 2], AF.Sqrt, bias=LN_EPS,
            reads=[("mv", sb)], writes=[("rs", sb)], name="ln_sqrt")

    def ln_part2(i):
        src_ap, grp, t4 = tiles[i]
        xb = i % 4
        sb = i % 3
        par = grp % 2
        ntile = 4 if grp < 5 else 2
        S.op("dve", lambda e: e.reciprocal(rs[:, sb:sb + 1], rs[:, sb:sb + 1]),
             reads=[("rs", sb)], writes=[("rs", sb)], name="ln_rstd")
        ts("dve", nmr[:, sb:sb + 1], mv[:, sb * 2: sb * 2 + 1], rs[:, sb:sb + 1], -1.0, ALU.mult, ALU.mult,
           reads=[("mv", sb), ("rs", sb)], writes=[("nmr", sb)])
        act(xn[par][t4], xt[xb], AF.Identity, bias=nmr[:, sb:sb + 1], scale=rs[:, sb:sb + 1],
            reads=[("xt", xb), ("rs", sb), ("nmr", sb)], writes=[("xn", par, t4)], name="ln_norm")
        for kc in range(8):
            b = par * 4 + kc // 2
            o_ap = bkbf(b)[:, (kc % 2) * 512 + t4 * 128:(kc % 2) * 512 + (t4 + 1) * 128]
            S.op("pe", lambda e, o_ap=o_ap, i_ap=xn[par][t4][:, kc * 128:(kc + 1) * 128]:
                 e.transpose(o_ap, i_ap, ident),
                 reads=[("xn", par, t4), "ident"], writes=[PB(b)], name="transpose")
        if t4 != ntile - 1:
            return
        if grp == 0:
            emit_ada()
        for kc in range(8):
            b = par * 4 + kc // 2
            src = bkbf(b)[:, (kc % 2) * 512:(kc % 2) * 512 + ntile * 128]
            j = 0 if grp < 5 else 1
            if grp < 5:
                dst = hxT[:, kc, grp * 512:(grp + 1) * 512]
                wr = ("hxT", grp, kc)
            else:
                dst = hcT[:, kc, :]
                wr = "hcT"
            if (kc // 2) % 2 == 0:
                ts("dve", dst, src, sc1(kc, j), sh(kc, j), ALU.mult, ALU.add,
                   reads=[PB(b), "ada"], writes=[wr], name="mod_evac")
            else:
                act(dst, src, AF.Identity, bias=sh(kc, j), scale=sc1(kc, j),
                    reads=[PB(b), "ada"], writes=[wr], name="mod_evac")
        if grp == 5:
            emit_g()

    NT = len(tiles)
    P0OFF = ARENA - 12288 - 8192 - 5632 - 64
    assert cur[0] <= P0OFF, ("opening phase overlaps pair-0 prefetch region", cur[0])
    Wp0 = view(P0OFF, [8, 512], BF16)
    Eb0 = view(P0OFF + 8192, [2, NTAB * 128], BF16)
    w_in_v0 = w_in.rearrange("(kc p) n -> p kc n", p=128)
    btab_v0 = btab.rearrange("c p (h n) -> c p h n", h=2)

    def prefetch_pair0():
        for j, base in enumerate((2 * D, 3 * D, 4 * D, 5 * D)):
            dma("pool", Wp0[:, :, j * 128:(j + 1) * 128], w_in_v0[:, :, base: base + 128], "wp0_%d" % j,
                writes=[("Wp", 0, j)], name="ld_wp")
        dma("pool", Eb0, btab_v0[0], "eb0", writes=[("Eb", 0)], name="ld_eb")

    ln_part1(0)
    ln_part1(1)
    for i in range(NT):
        if i + 2 < NT:
            ln_part1(i + 2)
        ln_part2(i)
        if i == 12:
            prefetch_pair0()
        if i == 15:
            emit_g_load()
    dbg_dump("hxT", hxT, [("hxT", g_, k_) for g_ in range(5) for k_ in range(8)])
    dbg_dump("hcT", hcT, ["hcT"])
    if stop == "A":
        return finish()

    S.barrier()
    cur[0] = ph0

    phB = cur[0]
    ctab = [tile([512], F32) for _ in range(2)]
    stab = [tile([512], F32) for _ in range(2)]
    Wp = [Wp0, tile([8, 512], BF16)]
    Eb = [Eb0, tile([2, NTAB * 128], BF16)]
    kT = tile([NSLOT], BF16)
    krot = tile([NSLOT], BF16)
    qP = tile([2, NOWN], BF16)
    qrotP = tile([2, NOWN], BF16)
    sz = tile([NOWN], BF16)
    Vaug = tile([20, 193], BF16)
    Vctx = tile([2, 193], BF16)
    kctxT = tile([256], BF16)
    t1 = [tile([512], F32) for _ in range(2)]
    t2 = [tile([512], F32) for _ in range(2)]
    PTW = [tile([1024], BF16) for _ in range(5)]
    rdf = [tile([512], F32) for _ in range(2)]
    rbc = [tile([512], F32) for _ in range(2)]
    g2 = [tile([512], F32) for _ in range(2)]
    tq = [tile([512], F32) for _ in range(2)]

    TOPO = ARENA - 12288
    assert cur[0] <= P0OFF, ("phase B overlaps pair-0 prefetch region", cur[0])
    Wq0 = view(TOPO, [8, 512], BF16)
    wpl = view(TOPO + 8192, [8, 256], BF16)
    w_pool_v = w_pool.rearrange("(g k p) n -> p (g k) n", g=4, k=2)
    for Qt in (qP, qrotP):
        mset("pool", Qt[64:128, 0, :], 0.0, writes=["qpad"])
        mset("pool", Qt[0:64, 1, :], 0.0, writes=["qpad"])
    for Vt in (Vaug, Vctx):
        mset("dve", Vt[:, :, 64:66], 1.0, writes=["Vconst"])
        mset("dve", Vt[:, :, 66:129], 0.0, writes=["Vconst"])

    w_in_v = w_in.rearrange("(kc p) n -> p kc n", p=128)
    btab_v = btab.rearrange("c p (h n) -> c p h n", h=2)
    PROJ = [7, 0, 1, 2, 3, 4, 5]
    pj = [0]

    def proj_bank():
        b = PROJ[pj[0] % len(PROJ)]
        pj[0] += 1
        return b

    def load_pair(c):
        buf = c % 2
        for j, base in enumerate((2 * D, 3 * D, 4 * D, 5 * D)):
            dma("pool", Wp[buf][:, :, j * 128:(j + 1) * 128],
                w_in_v[:, :, base + c * 128: base + (c + 1) * 128], "wp%d_%d" % (buf, j),
                writes=[("Wp", buf, j)], name="ld_wp")
        dma("pool", Eb[buf], btab_v[c], "eb%d" % buf, writes=[("Eb", buf)], name="ld_eb")

    def exp_tab(c):
        buf = c % 2
        act(Eb[buf], Eb[buf], AF.Exp, reads=[("Eb", buf)], writes=[("Eb", buf)], name="exp_tab")

    ropei = [0]

    def rope_tabs(off):
        i = ropei[0]
        ropei[0] += 1
        dma("sp", ctab[i % 2], ropeC[:, off:off + 512], "ctab%d" % (i % 2), writes=[("ctab", i % 2)], name="ld_rope")
        dma("sp", stab[i % 2], ropeS[:, off:off + 512], "stab%d" % (i % 2), writes=[("stab", i % 2)], name="ld_rope")
        return i % 2

    nbatch = [0]
    carry = []
    for c in range(npairs):
        buf = c % 2
        W = Wp[buf]
        if c + 1 < npairs:
            load_pair(c + 1)
        else:
            dma("pool", wpl, w_pool_v, "wpl", writes=["wpl"])
            dma("pool", Wq0[:, :, 0:256], w_in_v[:, :, 0:256], "wq0_0", writes=[("Wq", 0, 0)])
            dma("pool", Wq0[:, :, 256:512], w_in_v[:, :, D:D + 256], "wq0_1", writes=[("Wq", 0, 1)])
        pend = []

        def flush():
            while pend:
                pend.pop(0)()

        def k_step(g):
            ti = rope_tabs(g * 512)
            b = proj_bank()
            for kc in range(8):
                mm(bk(b), W[:, kc, 128:256], hxT[:, kc, g * 512:(g + 1) * 512], kc == 0, kc == 7,
                   reads=[("Wp", buf, 1), ("hxT", g, kc)], writes=[PB(b)], name="k_mm")
            sl = slice(g * 512, (g + 1) * 512)
            act(kT[:, sl], bk(b), AF.Identity, bias=b_in_sb[:, 24 + c:25 + c],
                reads=[PB(b), "b_in_sb"], writes=[("kT", g)], name="k_evac")

            def later():
                rb = proj_bank()
                mm(bk(rb), perm, kT[:, sl], True, True, reads=["perm", ("kT", g)], writes=[PB(rb)], name="rope_perm")
                tt("dve", t2[ti], bk(rb), stab[ti], ALU.mult, reads=[PB(rb), ("stab", ti)], writes=[("t2", ti)])
                tt("dve", t1[ti], kT[:, sl], ctab[ti], ALU.mult, reads=[("kT", g), ("ctab", ti)],
                   writes=[("t1", ti, 0), ("t1", ti, 1)])
                tt("dve", krot[:, sl], t1[ti], t2[ti], ALU.add,
                   reads=[("t1", ti, 0), ("t1", ti, 1), ("t2", ti)], writes=[("krot", g)])
            return later

        def q_step(g):
            ti = rope_tabs(OWN0 + g * 512)
            b = proj_bank()
            osl = slice(OWN0 + g * 512, OWN0 + (g + 1) * 512)
            sl = slice(g * 512, (g + 1) * 512)
            for kc in range(8):
                mm(bk(b), W[:, kc, 0:128], hxT[:, kc, osl], kc == 0, kc == 7,
                   reads=[("Wp", buf, 0), ("hxT", g, kc), ("hxT", g + 1, kc)], writes=[PB(b)], name="q_mm")
            for hh in range(2):
                hp = slice(64 * hh, 64 * hh + 64)
                act(qP[hp, hh, sl], bk(b)[hp, :], AF.Identity, bias=bq8[hp, c:c + 1], scale=0.125,
                    reads=[PB(b), "bq8", "qpad"], writes=[("qP", g, hh)], name="q_evac")

            def later():
                rb = proj_bank()
                for hh in range(2):
                    mm(bk(rb), perm, qP[:, hh, sl], hh == 0, hh == 1, reads=["perm", ("qP", g, hh), "qpad"],
                       writes=[PB(rb)], name="rope_perm")
                tt("dve", t2[ti], bk(rb), stab[ti], ALU.mult, reads=[PB(rb), ("stab", ti)], writes=[("t2", ti)])
                for hh in range(2):
                    hp = slice(64 * hh, 64 * hh + 64)
                    tt("dve", t1[ti][hp, :], qP[hp, hh, sl], ctab[ti][hp, :], ALU.mult,
                       reads=[("qP", g, hh), ("ctab", ti)], writes=[("t1", ti, hh)])
                    tt("dve", qrotP[hp, hh, sl], t1[ti][hp, :], t2[ti][hp, :], ALU.add,
                       reads=[("t1", ti, hh), ("t2", ti), "qpad"], writes=[("qrotP", g, hh)])
            return later

        def z_step(g):
            b = proj_bank()
            osl = slice(OWN0 + g * 512, OWN0 + (g + 1) * 512)
            sl = slice(g * 512, (g + 1) * 512)
            for kc in range(8):
                mm(bk(b), W[:, kc, 384:512], hxT[:, kc, osl], kc == 0, kc == 7,
                   reads=[("Wp", buf, 3), ("hxT", g, kc), ("hxT", g + 1, kc)], writes=[PB(b)], name="za_mm")
            act(sz[:, sl], bk(b), AF.Silu, bias=b_in_sb[:, 40 + c:41 + c],
                reads=[PB(b), "b_in_sb"], writes=[("sz", g)], name="za_evac")
            return None

        def v_step(g):
            b = proj_bank()
            for t4 in range(4):
                t = g * 4 + t4
                for kc in range(8):
                    mm(bk(b)[:, t4 * 128:(t4 + 1) * 128], hxT[:, kc, t * 128:(t + 1) * 128], W[:, kc, 256:384],
                       kc == 0, kc == 7, reads=[("Wp", buf, 2), ("hxT", g, kc)], writes=[PB(b)], name="v_mm")
            for hh in range(2):
                src = bk(b).rearrange("p (t h d) -> p t h d", t=4, h=2)[:, :, hh, :]
                dst = Vaug[:, g * 4:(g + 1) * 4, hh * 129: hh * 129 + 64]
                if g % 2 == 0:
                    cp("dve", dst, src, reads=[PB(b)], writes=[("Vaug", g, hh)], name="v_evac")
                else:
                    S.op("act", lambda e, dst=dst, src=src: e.copy(dst, src), reads=[PB(b)],
                         writes=[("Vaug", g, hh)], name="v_evac")
            return None

        def ctx_step():
            b = proj_bank()
            for kc in range(8):
                mm(bk(b)[:, 0:256], W[:, kc, 128:256], hcT[:, kc, :], kc == 0, kc == 7,
                   reads=[("Wp", buf, 1), "hcT"], writes=[PB(b)], name="kc_mm")
            act(kctxT, bk(b)[:, 0:256], AF.Identity, bias=b_in_sb[:, 24 + c:25 + c],
                reads=[PB(b), "b_in_sb"], writes=["kctxT"], name="kc_evac")
            b = proj_bank()
            for m in range(2):
                for kc in range(8):
                    mm(bk(b)[:, m * 128:(m + 1) * 128], hcT[:, kc, m * 128:(m + 1) * 128], W[:, kc, 256:384],
                       kc == 0, kc == 7, reads=[("Wp", buf, 2), "hcT"], writes=[PB(b)], name="vc_mm")
            for hh in range(2):
                src = bk(b)[:, 0:256].rearrange("p (t h d) -> p t h d", t=2, h=2)[:, :, hh, :]
                dst = Vctx[:, :, hh * 129: hh * 129 + 64]
                cp("dve", dst, src, reads=[PB(b)], writes=["Vctx"], name="vc_evac")
            return None

        steps = [(k_step, g) for g in range(5)] + [(q_step, g) for g in range(4)] + \
                [(z_step, g) for g in range(4)] + [(v_step, g) for g in range(5)] + [(ctx_step, None)]
        for si_, (fnc, arg) in enumerate(steps):
            lat = fnc(arg) if arg is not None else fnc()
            flush()
            if lat is not None:
                pend.append(lat)
            if si_ == 6:
                while carry:
                    carry.pop(0)()
        flush()
        while carry:
            carry.pop(0)()

        WAVES = [
            [("ctx", 0, 0), ("ctx", 1, 512)],
            [("loc", 4, 0), ("loc", 3, 512)],
            [("loc", 5, 0), ("loc", 7, 384), ("loc", 2, 512), ("loc", 0, 896)],
            [("loc", 6, 0), ("loc", 1, 256)],
        ]
        MERGE = [[], [(0, 1)], [(0, 2), (1, 3)], [(0, 1)]]
        WIDTH = [1024, 1024, 1024, 512]
        items = [(hh, qb, wi) for hh in range(2) for qb in range(4) for wi in range(4)]

        def seg_info(kind, r, j0):
            if kind == "ctx":
                return 0, 3
            return max(0, r - 4), min(3, r)

        def emit_qk(n):
            hh, qb, wi = items[n]
            hp = slice(64 * hh, 64 * hh + 64)
            j0 = qb * 4
            slot = n % 3
            base = slot * 1024
            pt = PTW[n % 5]
            ptr = ("PTW", n % 5)
            for (kind, r, off) in WAVES[wi]:
                lo, hi = seg_info(kind, r, j0)
                nq_ = (hi - lo + 1) * 128
                bnk = (base + off) // 512
                o_ap = psall[:, base + off: base + off + nq_]
                qs = slice((j0 + lo) * 128, (j0 + hi + 1) * 128)
                if kind == "loc":
                    kt = j0 + r
                    mm(o_ap, krot[:, kt * 128:(kt + 1) * 128], qrotP[:, hh, qs], True, False,
                       reads=[("krot", kt // 4), ("qrotP", qb, hh), "qpad"], writes=[PB(bnk)], name="qk")
                    ids = [SPECIAL.get((j0 + s_, r - s_ - 2), 2 - (r - s_ - 2)) for s_ in range(lo, hi + 1)]
                    runs = []
                    st = 0
                    for i in range(1, len(ids) + 1):
                        if i == len(ids) or ids[i] != ids[i - 1] + 1:
                            runs.append((st, i))
                            st = i
                    for ri, (a0, a1) in enumerate(runs):
                        b_ap = Eb[buf][:, hh, ids[a0] * 128:(ids[a0] + a1 - a0) * 128]
                        mm(psall[:, base + off + a0 * 128: base + off + a1 * 128], ident, b_ap, False,
                           ri == len(runs) - 1, reads=["ident", ("Eb", buf)], writes=[PB(bnk)], name="bias_mm")
                else:
                    mm(o_ap, kctxT[:, r * 128:(r + 1) * 128], qP[:, hh, qs], True, True,
                       reads=["kctxT", ("qP", qb, hh), "qpad"], writes=[PB(bnk)], name="qkc")
            wd = WIDTH[wi]
            bset = [PB(slot * 2)] + ([PB(slot * 2 + 1)] if wd > 512 else [])
            act(pt[:, 0:wd], psall[:, base: base + wd], AF.Exp, reads=bset, writes=[ptr], name="exp")

        deferred = []

        def emit_pv(n):
            hh, qb, wi = items[n]
            hp = slice(64 * hh, 64 * hh + 64)
            vcol = slice(0, 65) if hh == 0 else slice(65, 193)
            xrows = slice(0, 65) if hh == 0 else slice(0, 128)
            j0 = qb * 4
            bi = nbatch[0]
            XB = 6 + (bi % 2)
            pt = PTW[n % 5]
            ptr = ("PTW", n % 5)
            segs = WAVES[wi]
            for si, (kind, r, off) in enumerate(segs):
                lo, hi = seg_info(kind, r, j0)
                nq_ = (hi - lo + 1) * 128
                if kind == "loc":
                    kt = j0 + r
                    lhsT = Vaug[:, kt, vcol]
                    rdv = ("Vaug", kt // 4, hh)
                else:
                    lhsT = Vctx[:, r, vcol]
                    rdv = "Vctx"
                mm(bk(XB)[xrows, lo * 128:(hi + 1) * 128], lhsT, pt[:, off:off + nq_],
                   wi == 0 and si == 0, wi == 3 and si == len(segs) - 1,
                   reads=[rdv, "Vconst", ptr], writes=[PB(XB)], name="pv")
            if wi != 3:
                return
            nbatch[0] += 1
            gi = bi % 2
            drow = slice(64, 65) if hh == 0 else slice(0, 1)
            act(rdf[gi][drow, :], bk(XB)[drow, :], AF.Ln, reads=[PB(XB)], writes=[("rdf", gi)], name="den_ln")
            cp("dve", tq[gi][hp, :], bk(XB)[hp, :], reads=[PB(XB)], writes=[("tq", gi)], name="o_copy")
            act(rdf[gi][drow, :], rdf[gi][drow, :], AF.Exp, scale=-1.0, reads=[("rdf", gi)], writes=[("rdf", gi)],
                name="den_rcp")
            dma("sp", rd_dram[bi:bi + 1, :], rdf[gi][drow, :], "rdw%d" % gi, reads=[("rdf", gi)],
                writes=[("rd_dram", bi)], name="rd_st")
            dma("sp", rbc[gi][hp, :], rd_dram[bi:bi + 1, :].partition_broadcast(64), "rdr%d" % gi,
                reads=[("rd_dram", bi)], writes=[("rbc", gi)], name="rd_ld")
            qsl = slice(qb * 512, (qb + 1) * 512)

            def later(cc=c):
                tt("dve", tq[gi][hp, :], tq[gi][hp, :], rbc[gi][hp, :], ALU.mult,
                   reads=[("tq", gi), ("rbc", gi)], writes=[("tq", gi)], name="tq")
                stt_op("dve", naT[hp, cc, qsl], tq[gi][hp, :], b_in_sb[hp, 32 + cc:33 + cc], sz[hp, qsl],
                       ALU.add, ALU.mult, reads=[("sz", qb), ("tq", gi), "b_in_sb"], writes=[("naT", cc)], name="na")
            deferred.append((n + 6, later))

        nq = 0
        NI = len(items)
        for n in range(NI):
            while nq < min(n + 3, NI):
                emit_qk(nq)
                nq += 1
            emit_pv(n)
            while deferred and deferred[0][0] <= n:
                deferred.pop(0)[1]()
        while deferred:
            carry.append(deferred.pop(0)[1])
        if c == npairs - 1:
            while carry:
                carry.pop(0)()

    dbg_dump("naT", naT, [("naT", c_) for c_ in range(8)])
    dbg_dump("krot", krot, [("krot", g_) for g_ in range(5)])
    dbg_dump("sz", sz, [("sz", g_) for g_ in range(4)])
    dbg_dump("Vaug", Vaug, [("Vaug", g_, h_) for g_ in range(5) for h_ in range(2)])
    if stop == "B":
        return finish()
    S.barrier()

    cur[0] = phB
    poolT = tile([8, NOWN], BF16)
    wout_lo = tile([8, D], BF16)
    after_poolT = cur[0]
    Wq = [Wq0, tile([8, 512], BF16)]
    ub = [tile([2080], F32) for _ in range(2)]
    pa = tile([2080], F32)
    pb_ = tile([2080], F32)
    pooled = tile([2, NOWN], BF16)
    bt = tile([32], F32)

    def load_group(g):
        buf = g % 2
        dma("pool", Wq[buf][:, :, 0:256], w_in_v[:, :, 256 * g:256 * (g + 1)], "wq%d_0" % buf,
            writes=[("Wq", buf, 0)])
        dma("pool", Wq[buf][:, :, 256:512], w_in_v[:, :, D + 256 * g: D + 256 * (g + 1)], "wq%d_1" % buf,
            writes=[("Wq", buf, 1)])

    if npairs < 1:
        load_group(0)
    w_out_v = w_out.rearrange("(kc p) n -> p kc n", p=128)
    for q in range(2):
        dma("pool", wout_lo[:, 4 * q:4 * q + 4, :], w_out_v[:, 4 * q:4 * q + 4, :], "wout%d" % q, writes=[("wout", q)])
    PROJ[:] = [0, 1, 2, 3, 4, 5, 6, 7]
    szp8 = [tile([512], BF16) for _ in range(8)]
    assert cur[0] <= TOPO, ("pool phase overlaps top region", cur[0])
    for g in range(npool):
        buf = g % 2
        W = Wq[buf]
        if g + 1 < npool:
            load_group(g + 1)
        w = 2 ** (g + 1)
        N = 2080
        for e2 in range(2):
            pc = 2 * g + e2
            u = ub[e2]
            ur = ("ub", e2)
            bias = b_in_sb[:, pc:pc + 1]
            for g4 in range(4):
                b = proj_bank()
                for kc in range(8):
                    mm(bk(b), W[:, kc, e2 * 128:(e2 + 1) * 128], hxT[:, kc, OWN0 + g4 * 512: OWN0 + (g4 + 1) * 512],
                       kc == 0, kc == 7, reads=[("Wq", buf, 0), ("hxT", g4, kc), ("hxT", g4 + 1, kc)], writes=[PB(b)],
                       name="u_mm")
                act(u[:, 16 + g4 * 512: 16 + (g4 + 1) * 512], bk(b), AF.Identity, bias=bias,
                    reads=[PB(b), "b_in_sb"], writes=[(ur, g4)], name="u_evac")
            b = proj_bank()
            for side, s0 in enumerate((OWN0 - 16, OWN0 + NOWN)):
                for kc in range(8):
                    mm(bk(b)[:, side * 16:(side + 1) * 16], W[:, kc, e2 * 128:(e2 + 1) * 128], hxT[:, kc, s0:s0 + 16],
                       kc == 0, kc == 7, reads=[("Wq", buf, 0), ("hxT", 0, kc), ("hxT", 4, kc)], writes=[PB(b)],
                       name="uh_mm")
            for side, c0 in enumerate((0, 16 + NOWN)):
                act(u[:, c0:c0 + 16], bk(b)[:, side * 16:(side + 1) * 16], AF.Identity, bias=bias,
                    reads=[PB(b), "b_in_sb"], writes=[(ur, 4 + side)], name="uh_evac")
                tt("dve", u[:, c0:c0 + 16], u[:, c0:c0 + 16], umask[:, side * 16:(side + 1) * 16], ALU.mult,
                   reads=[(ur, 4 + side), "umask"], writes=[(ur, 4 + side)], name="uh_mask")
        for oc in range(2):
            pc = 2 * g + oc
            for g4 in range(4):
                zb = proj_bank()
                for kc in range(8):
                    mm(bk(zb), W[:, kc, 256 + oc * 128: 256 + (oc + 1) * 128],
                       hxT[:, kc, OWN0 + g4 * 512: OWN0 + (g4 + 1) * 512], kc == 0, kc == 7,
                       reads=[("Wq", buf, 1), ("hxT", g4, kc), ("hxT", g4 + 1, kc)], writes=[PB(zb)], name="zp_mm")
                act(szp8[oc * 4 + g4], bk(zb), AF.Silu, bias=b_in_sb[:, 8 + pc:9 + pc], reads=[PB(zb), "b_in_sb"],
                    writes=[("szp", oc * 4 + g4)], name="zp_evac")
        for e2 in range(2):
            u = ub[e2]
            ur = ("ub", e2)
            pa_, pan = pa, "pa"
            pb2, pbn = pb_, "pb"
            ue = [(ur, k) for k in range(6)]
            tt("dve", pa_[:, 1:N], u[:, 0:N - 1], u[:, 1:N], ALU.add, reads=ue, writes=[pan], name="s2")
            res, resn = pa_, pan
            if g >= 1:
                tt("dve", pb2[:, 2:N - 1], pa_[:, 1:N - 2], pa_[:, 3:N], ALU.add, reads=[pan], writes=[pbn], name="s4")
                res, resn = pb2, pbn
            if g >= 2:
                tt("dve", pa_[:, 4:N - 3], pb2[:, 2:N - 5], pb2[:, 6:N - 1], ALU.add, reads=[pbn], writes=[pan], name="s8")
                res, resn = pa_, pan
            if g >= 3:
                tt("dve", pb2[:, 8:N - 7], pa_[:, 4:N - 11], pa_[:, 12:N - 3], ALU.add, reads=[pan], writes=[pbn], name="s16")
                res, resn = pb2, pbn
            stt_op("dve", pooled[:, e2, :], res[:, 16:16 + NOWN], 1.0 / w, u[:, 16:16 + NOWN], ALU.mult, ALU.subtract,
                   reads=[resn] + ue, writes=[("pooled", e2)], name="pool_sub")
            for side, c0 in enumerate((0, NOWN - 16)):
                iv = invc[:, (g * 2 + side) * 16:(g * 2 + side + 1) * 16]
                tt("dve", bt[:, e2 * 16:(e2 + 1) * 16], res[:, 16 + c0:32 + c0], iv, ALU.mult, reads=[resn, "invc"],
                   writes=[("bt", e2)], name="edge")
                tt("dve", pooled[:, e2, c0:c0 + 16], bt[:, e2 * 16:(e2 + 1) * 16], u[:, 16 + c0:32 + c0], ALU.subtract,
                   reads=[("bt", e2)] + ue, writes=[("pooled", e2)], name="edge")
        for oc in range(2):
            pc = 2 * g + oc
            for g4 in range(4):
                lb = proj_bank()
                for k2 in range(2):
                    mm(bk(lb), wpl[:, g * 2 + k2, oc * 128:(oc + 1) * 128], pooled[:, k2, g4 * 512:(g4 + 1) * 512],
                       k2 == 0, k2 == 1, reads=["wpl", ("pooled", 0), ("pooled", 1)], writes=[PB(lb)], name="pl_mm")
                stt_op("dve", poolT[:, pc, g4 * 512:(g4 + 1) * 512], bk(lb), pscale[:, pc:pc + 1], szp8[oc * 4 + g4],
                       ALU.mult, ALU.mult, reads=[PB(lb), "pscale", ("szp", oc * 4 + g4)], writes=[("poolT", pc)],
                       name="pl_evac")

    dbg_dump("poolT", poolT, [("poolT", c_) for c_ in range(8)])
    if stop == "C1":
        return finish()
    S.barrier()

    cur[0] = after_poolT
    wout_hi = tile([8, D], BF16)
    lng = tile([D], F32)
    lnb = tile([D], F32)
    bo_f = tile([D], F32)
    bo_b = tile([D], BF16)
    xr = [tile([D], F32) for _ in range(2)]
    ty = [tile([D], F32) for _ in range(2)]
    zz = [tile([D], F32) for _ in range(2)]
    ot = [tile([D], F32) for _ in range(2)]
    for q in range(2, 4):
        dma("pool", wout_hi[:, 4 * q - 8:4 * q - 4, :], w_out_v[:, 4 * q:4 * q + 4, :], "wout%d" % q, writes=[("wout", q)])
    dma("sp", lng, ln_g.partition_broadcast(128), "c_lng", writes=["lng"])
    dma("sp", lnb, ln_b.partition_broadcast(128), "c_lnb", writes=["lnb"])
    dma("sp", bo_f[0:1, :], b_out, "c_bo", writes=["bo_f"])
    cp("dve", bo_b[0:1, :], bo_f[0:1, :], reads=["bo_f"], writes=["bo_b"])

    xr3 = xr + [tile([D], F32)]

    def c2_load(t):
        dma("sp", xr3[t % 3], xs[OWN0 + t * 128: OWN0 + (t + 1) * 128, :], "xr%d" % (t % 3), writes=[("xr", t % 3)])

    zz3 = [view(hx_off + i_ * 4096, [D], F32) for i_ in range(3)]
    ty3 = [view(hx_off + 12288 + i_ * 4096, [D], F32) for i_ in range(3)]

    def c2_a(t):
        par = t % 2
        p3 = t % 3
        sb = t % 3
        tsl = slice(t * 128, (t + 1) * 128)
        for nh in range(2):
            b = par * 2 + nh
            for kc in range(16):
                lhsT = poolT[:, kc, tsl] if kc < 8 else naT[:, kc - 8, tsl]
                rdr = ("poolT", kc) if kc < 8 else ("naT", kc - 8)
                wsl = wout_lo[:, kc, nh * 512:(nh + 1) * 512] if kc < 8 else wout_hi[:, kc - 8, nh * 512:(nh + 1) * 512]
                mm(bk(b), lhsT, wsl, kc == 0, False,
                   reads=[rdr, ("wout", kc // 4)], writes=[PB(b)], name="out_mm")
            mm(bk(b), ones_bf[0:1, :], bo_b[0:1, nh * 512:(nh + 1) * 512], False, True,
               reads=["ones_bf", "bo_b"], writes=[PB(b)], name="out_bias")
        tt("dve", ty3[p3], psall[:, par * 1024:(par + 1) * 1024], g_bc, ALU.mult,
           reads=[PB(par * 2), PB(par * 2 + 1), "g_bc"], writes=[("ty3", p3)], name="gate_y")
        stt_op("dve", zz3[p3], xr3[t % 3], ALPHA, ty3[p3], ALU.mult, ALU.add,
               reads=[("xr", t % 3), ("ty3", p3)], writes=[("zz3", p3)], name="resid")
        for hf in range(2):
            S.op("dve", lambda e, hf=hf: e.bn_stats(
                stt[:, sb * 12 + hf * 6: sb * 12 + hf * 6 + 6], zz3[p3][:, hf * 512:(hf + 1) * 512]),
                reads=[("zz3", p3)], writes=[("stt", sb)], name="bn_stats")
        S.op("dve", lambda e: e.bn_aggr(mv[:, sb * 2: sb * 2 + 2], stt[:, sb * 12: sb * 12 + 12]),
             reads=[("stt", sb)], writes=[("mv", sb)], name="bn_aggr")
        act(rs[:, sb:sb + 1], mv[:, sb * 2 + 1: sb * 2 + 2], AF.Sqrt, bias=LN_EPS,
            reads=[("mv", sb)], writes=[("rs", sb)], name="ln_sqrt")

    def c2_b1(t):
        p3 = t % 3
        sb = t % 3
        S.op("dve", lambda e: e.reciprocal(rs[:, sb:sb + 1], rs[:, sb:sb + 1]),
             reads=[("rs", sb)], writes=[("rs", sb)], name="ln_rstd")
        ts("dve", nmr[:, sb:sb + 1], mv[:, sb * 2: sb * 2 + 1], rs[:, sb:sb + 1], -1.0, ALU.mult, ALU.mult,
           reads=[("mv", sb), ("rs", sb)], writes=[("nmr", sb)])
        act(zz3[p3], zz3[p3], AF.Identity, bias=nmr[:, sb:sb + 1], scale=rs[:, sb:sb + 1],
            reads=[("zz3", p3), ("rs", sb), ("nmr", sb)], writes=[("zz3", p3)], name="ln2")

    def c2_b2(t):
        par = t % 2
        p3 = t % 3
        tsl = slice(t * 128, (t + 1) * 128)
        tt("dve", ty3[p3], zz3[p3], lng, ALU.mult, reads=[("zz3", p3), "lng"], writes=[("ty3", p3)], name="ln_g")
        tt("dve", ot[par], ty3[p3], lnb, ALU.add, reads=[("ty3", p3), "lnb"], writes=[("ot", par)], name="ln_b")
        dma("sp", out[tsl, :], ot[par], "out%d" % par, reads=[("ot", par)], name="store")

    c2_load(0)
    c2_load(1)
    c2_a(0)
    for t in range(16):
        if t + 2 < 16:
            c2_load(t + 2)
        if t + 1 < 16:
            c2_a(t + 1)
        c2_b1(t)
        if t >= 1:
            c2_b2(t - 1)
    c2_b2(15)

    return finish()


_CACHE = {}


def _prepare_inputs(x, c, ctx, c_ctx, w_ada, b_ada, w_in, b_in, w_pool, pool_scale, rpb,
                    w_out, b_out, ln_g, ln_b):
    f = lambda a: np.ascontiguousarray(np.asarray(a, dtype=np.float32))
    x, c, ctx, c_ctx = f(x), f(c), f(ctx), f(c_ctx)
    w_ada, b_ada, w_in, b_in = f(w_ada)[0], f(b_ada)[0], f(w_in)[0], f(b_in)[0]
    w_pool, pool_scale, rpb = f(w_pool)[0], f(pool_scale)[0], f(rpb)[0]
    w_out, b_out, ln_g, ln_b = f(w_out)[0], f(b_out)[0], f(ln_g)[0], f(ln_b)[0]
    shared = {
        "w_ada": w_ada, "b_ada_t": _tvec(b_ada, 24), "b_ada_g": b_ada[2 * D:3 * D].reshape(1, D).copy(),
        "w_in": w_in, "b_in_t": _tvec(b_in, 48),
        "w_pool": np.ascontiguousarray(w_pool.reshape(4 * 256, 256)), "pscale_t": _tvec(pool_scale, 8),
        "w_out": w_out, "b_out": b_out.reshape(1, D).copy(), "ln_g": ln_g.reshape(1, D).copy(),
        "ln_b": ln_b.reshape(1, D).copy(),
        "perm": _perm_matrix(), "ident": np.eye(128, dtype=np.float32),
    }
    in_maps = []
    for i in range(NCORES):
        b = i // 4
        r0 = (i % 4) * 32
        srow = _slot_rows(r0)
        xg = x[b].reshape(128, GW, D)[srow].reshape(NSLOT, D)
        cv = np.empty((128, 8, 2), np.float32)
        cv[:, :, 0] = c[b].reshape(8, 128).T
        cv[:, :, 1] = c_ctx.reshape(8, 128).T
        C, Sg = _rope_tables(r0)
        invc, um = _pool_tables(r0)
        m = dict(shared)
        m.update({
            "xs": np.ascontiguousarray(xg), "ctx": np.ascontiguousarray(ctx[b]),
            "cvec": np.ascontiguousarray(cv.reshape(128, 16)),
            "ropeC": C, "ropeS": Sg,
            "btab": np.ascontiguousarray(_bias_tables(rpb, r0).reshape(8, 128, 2 * NTAB * 128)),
            "invc": np.ascontiguousarray(invc.reshape(128, 128)),
            "umask": np.ascontiguousarray(um.reshape(128, 32)),
        })
        in_maps.append(m)
    return in_maps


def kernel(x, c, ctx, c_ctx, w_ada, b_ada, w_in, b_in, w_pool, pool_scale, rpb,
           w_out, b_out, ln_g, ln_b):
    in_maps = _prepare_inputs(x, c, ctx, c_ctx, w_ada, b_ada, w_in, b_in, w_pool, pool_scale, rpb,
                              w_out, b_out, ln_g, ln_b)
    if "nc" not in _CACHE:
        _CACHE["nc"] = build_program()[0]
    nc = _CACHE["nc"]
    res = run_bass_kernel_spmd(nc, in_maps, core_ids=list(range(NCORES)))
    outs = [np.asarray(r["out"], dtype=np.float32).reshape(NOWN, D) for r in res.results]
    full = np.stack([np.concatenate(outs[0:4], axis=0), np.concatenate(outs[4:8], axis=0)], axis=0)
    return full.astype(np.float32)
```

```python
import numpy as np
import ml_dtypes
import concourse.bass as bass
import concourse.mybir as mybir
from concourse.bass_utils import run_bass_kernel_spmd

F32 = mybir.dt.float32
BF16 = mybir.dt.bfloat16
U8 = mybir.dt.uint8
ALU = mybir.AluOpType
AF = mybir.ActivationFunctionType

D = 1024
L = 8192
GW = 64
NSLOT = 2560
NOWN = 2048
OWN0 = 256
NCORES = 8
ALPHA = float((2.0 * 1) ** 0.25)
LN_EPS = 1e-6
NEG = -30000.0
NTAB = 11
SPECIAL = {(0, -2): 5, (0, -1): 6, (1, -2): 7, (14, 2): 8, (15, 1): 9, (15, 2): 10}

DEBUG_OUT = None


class Op:
    __slots__ = ("eng", "fn", "deps", "chan", "dmaval", "sig", "sigval", "name")

    def __init__(self, eng, fn, chan, name):
        self.eng = eng
        self.fn = fn
        self.deps = set()
        self.chan = chan
        self.dmaval = 0
        self.sig = False
        self.sigval = 0
        self.name = name


class Sched:
    ENGS = ("pe", "act", "dve", "pool", "sp")

    def __init__(self):
        self.prog = {e: [] for e in self.ENGS}
        self.last_w = {}
        self.readers = {}
        self.chan_cnt = {}
        self.pending = {e: set() for e in self.ENGS}
        self.dma_ops = []

    def op(self, eng, fn, reads=(), writes=(), chan=None, name=""):
        o = Op(eng, fn, chan, name)
        deps = set()
        writes = list(writes) + [r for r in reads if isinstance(r, tuple) and r[0] == "psum"]
        reads = [r for r in reads if not (isinstance(r, tuple) and r[0] == "psum")]
        for r in reads:
            lw = self.last_w.get(r)
            if lw is not None:
                deps.add(lw)
        for w in writes:
            lw = self.last_w.get(w)
            if lw is not None:
                deps.add(lw)
            rd = self.readers.get(w)
            if rd:
                deps.update(rd.values())
        deps |= self.pending[eng]
        self.pending[eng] = set()
        if eng == "pe":
            deps = {d for d in deps if not (d.eng == "pe" and d.chan is None)}
        o.deps = deps
        for w in writes:
            self.last_w[w] = o
            self.readers[w] = {}
        for r in reads:
            key = eng if chan is None else ("dma", len(self.dma_ops))
            self.readers.setdefault(r, {})[key] = o
        if chan is not None:
            self.chan_cnt[chan] = self.chan_cnt.get(chan, 0) + 16
            o.dmaval = self.chan_cnt[chan]
            self.dma_ops.append(o)
        self.prog[eng].append(o)
        return o

    def barrier(self):
        lasts = set()
        for e in self.ENGS:
            for o in reversed(self.prog[e]):
                if o.chan is None:
                    lasts.add(o)
                    break
        latest = {}
        for o in self.dma_ops:
            latest[o.chan] = o
        lasts |= set(latest.values())
        for e in self.ENGS:
            self.pending[e] |= lasts

    def emit(self, nc):
        engobj = {"pe": nc.tensor, "act": nc.scalar, "dve": nc.vector, "pool": nc.gpsimd, "sp": nc.sync}
        for e in self.ENGS:
            for o in self.prog[e]:
                for d in o.deps:
                    if d.chan is None:
                        d.sig = True
        for e in self.ENGS:
            n = 0
            for o in self.prog[e]:
                if o.chan is None and o.sig:
                    n += 1
                    o.sigval = n
        esem = {e: nc.alloc_semaphore("es_" + e) for e in ("pe", "act", "dve", "pool")}
        csem = {c: nc.alloc_semaphore("cs_%d" % i) for i, c in enumerate(sorted(self.chan_cnt))}
        stats = {"waits": 0, "sigs": 0, "ops": 0}
        with nc.Block() as block:
            def run(e):
                def body(eng):
                    waited = {}
                    for o in self.prog[e]:
                        need = {}
                        for d in o.deps:
                            if d.chan is not None:
                                k = ("c", d.chan)
                                v = d.dmaval
                            else:
                                k = ("e", d.eng)
                                v = d.sigval
                            if v > need.get(k, 0):
                                need[k] = v
                        for k, v in need.items():
                            if v > waited.get(k, 0):
                                waited[k] = v
                                sem = csem[k[1]] if k[0] == "c" else esem[k[1]]
                                eng.wait_ge(sem, v)
                                stats["waits"] += 1
                        ins = o.fn(eng)
                        stats["ops"] += 1
                        if o.chan is not None:
                            ins.then_inc(csem[o.chan], 16)
                        elif o.sig:
                            ins.then_inc(esem[e], 1)
                            stats["sigs"] += 1
                    if e == "sp":
                        for c, v in self.chan_cnt.items():
                            eng.wait_ge(csem[c], v)
                return body
            block.tensor(run("pe"))
            block.scalar(run("act"))
            block.vector(run("dve"))
            block.gpsimd(run("pool"))
            block.sync(run("sp"))
        return stats


def _slot_rows(r0):
    g = np.arange(40) + r0 - 4
    g = np.where(g < 0, g + 8, g)
    g = np.where(g > 127, g - 8, g)
    return g


def _rope_tables(r0):
    srow = _slot_rows(r0)
    s = np.arange(NSLOT)
    grow = srow[s // GW].astype(np.float32)
    gcol = (s % GW).astype(np.float32)
    p = np.arange(128)
    d = p % 64
    isrow = d < 32
    dd = np.where(isrow, d, d - 32)
    i = dd % 16
    inv = (np.float32(10000.0) ** (-(i.astype(np.float32)) / np.float32(16.0))).astype(np.float32)
    pos = np.where(isrow[:, None], grow[None, :], gcol[None, :]).astype(np.float32)
    ang = (pos * inv[:, None]).astype(np.float32)
    C = np.cos(ang).astype(np.float32)
    Sn = np.sin(ang).astype(np.float32)
    S = np.where((dd < 16)[:, None], -Sn, Sn).astype(np.float32)
    return np.ascontiguousarray(C), np.ascontiguousarray(S)


def _perm_matrix():
    p = np.arange(128)
    partner = np.where((p % 32) < 16, p + 16, p - 16)
    m = np.zeros((128, 128), np.float32)
    m[partner, p] = 1.0
    return m


def _bias_tables(rpb, r0):
    srow = _slot_rows(r0)
    combos = [(8, o) for o in (2, 1, 0, -1, -2)] + sorted(SPECIAL, key=lambda k: SPECIAL[k])
    kp = np.arange(128)
    a = kp // 64
    kc = kp % 64
    qq = np.arange(128)
    b2 = qq // 64
    qc = qq % 64
    cs = np.clip(qc - 8, 0, GW - 16)
    colvalid = (kc[:, None] >= cs[None, :]) & (kc[:, None] < cs[None, :] + 16)
    dc = kc[:, None] - qc[None, :] + 15
    out = np.empty((16, 128, NTAB * 128), np.float32)
    for ti, (j, o) in enumerate(combos):
        kt = j + 2 + o
        sr = 2 * kt + a
        lr = 2 * j + b2
        rowvalid = (sr[:, None] >= lr[None, :]) & (sr[:, None] <= lr[None, :] + 7)
        gk = srow[sr]
        gq = r0 + lr
        dr = gk[:, None] - gq[None, :] + 7
        valid = rowvalid & colvalid
        drc = np.clip(dr, 0, 14)
        dcc = np.clip(dc, 0, 30)
        g = rpb[:, drc, dcc]
        out[:, :, ti * 128:(ti + 1) * 128] = np.where(valid[None], g, np.float32(NEG))
    out = out.reshape(8, 2, 128, NTAB * 128).transpose(0, 2, 1, 3)
    return np.ascontiguousarray(out)


def _pool_tables(r0):
    T0 = r0 * GW
    invc = np.empty((128, 4, 2, 16), np.float32)
    for g, w in enumerate((2, 4, 8, 16)):
        for side, base in enumerate((0, NOWN - 16)):
            tg = T0 + base + np.arange(16)
            lo = np.clip(tg - w // 2, 0, L)
            hi = np.clip(tg - w // 2 + w, 0, L)
            invc[:, g, side, :] = (np.float32(1.0) / (hi - lo).astype(np.float32))[None, :]
    um = np.empty((128, 2, 16), np.float32)
    um[:, 0, :] = ((T0 - 16 + np.arange(16)) >= 0).astype(np.float32)[None, :]
    um[:, 1, :] = ((T0 + NOWN + np.arange(16)) < L).astype(np.float32)[None, :]
    return invc, um


def _tvec(v, n):
    return np.ascontiguousarray(np.asarray(v, np.float32).reshape(n, 128).T)


def build_program(debug=None, stop=None, npairs=8, npool=4):
    nc = bass.Bass("TRN2", target_bir_lowering=False)
    S = Sched()

    def din(name, shape):
        return nc.dram_tensor(name, list(shape), F32, kind="ExternalInput").ap()

    xs = din("xs", [NSLOT, D])
    ctx = din("ctx", [256, D])
    cvec = din("cvec", [128, 16])
    w_ada = din("w_ada", [D, 3 * D])
    b_ada_t = din("b_ada_t", [128, 24])
    b_ada_g = din("b_ada_g", [1, D])
    w_in = din("w_in", [D, 6 * D])
    b_in_t = din("b_in_t", [128, 48])
    w_pool = din("w_pool", [4 * 256, 256])
    pscale_t = din("pscale_t", [128, 8])
    w_out = din("w_out", [2 * D, D])
    b_out = din("b_out", [1, D])
    ln_g = din("ln_g", [1, D])
    ln_b = din("ln_b", [1, D])
    ropeC = din("ropeC", [128, NSLOT])
    ropeS = din("ropeS", [128, NSLOT])
    perm_d = din("perm", [128, 128])
    ident_d = din("ident", [128, 128])
    btab = din("btab", [8, 128, 2 * NTAB * 128])
    invc_d = din("invc", [128, 128])
    umask_d = din("umask", [128, 32])
    out = nc.dram_tensor("out", [NOWN, D], F32, kind="ExternalOutput").ap()
    rd_dram = nc.dram_tensor("rd_scratch", [64, 512], F32).ap()
    dbg = {}
    if debug:
        for k, (shp, dt_) in debug.items():
            dbg[k] = nc.dram_tensor("dbg_" + k, list(shp), dt_, kind="ExternalOutput").ap()

    ARENA = 206 * 1024
    arena = nc.alloc_sbuf_tensor("arena", [128, ARENA], U8)
    cur = [0]

    def alloc(nbytes):
        off = cur[0]
        nb = (nbytes + 63) // 64 * 64
        cur[0] += nb
        assert cur[0] <= ARENA, ("SBUF arena overflow", cur[0])
        return off

    def view(off, shape, dt):
        esz = 4 if dt == F32 else 2
        n = int(np.prod(shape))
        ap = arena[:, off:off + n * esz].bitcast(dt)
        if len(shape) == 2:
            return ap.rearrange("p (a b) -> p a b", b=shape[1])
        if len(shape) == 3:
            return ap.rearrange("p (a b c) -> p a b c", b=shape[1], c=shape[2])
        return ap

    def tile(shape, dt):
        esz = 4 if dt == F32 else 2
        return view(alloc(int(np.prod(shape)) * esz), shape, dt)

    ident = tile([128], BF16)
    perm = tile([128], BF16)
    ones_bf = tile([128], BF16)
    selB = tile([128], BF16)
    cs_f = tile([16], F32)
    cs_b = tile([16], BF16)
    ada = tile([32], F32)
    b_ada_sb = tile([24], F32)
    b_in_sb = tile([48], F32)
    bq8 = tile([8], F32)
    pscale = tile([8], F32)
    invc = tile([128], F32)
    umask = tile([32], F32)
    stt = tile([3 * 12], F32)
    mv = tile([3 * 2], F32)
    rs = tile([3], F32)
    nmr = tile([3], F32)
    hx_off = (cur[0] + 63) // 64 * 64
    hxT = tile([8, NSLOT], BF16)
    hcT = tile([8, 256], BF16)
    naT = tile([8, NOWN], BF16)
    g_bc = tile([D], F32)
    persist_end = cur[0]

    psall_t = nc.alloc_psum_tensor("psall", [128, 4096], F32)
    psall = psall_t[:, :]

    def bk(i):
        return psall[:, i * 512:(i + 1) * 512]

    def bkbf(i):
        return psall[:, i * 512:(i + 1) * 512].bitcast(BF16)

    PB = lambda i: ("psum", i)

    def dma(eng, out_ap, in_ap, chan, reads=(), writes=(), name="dma"):
        return S.op(eng, lambda e: e.dma_start(out=out_ap, in_=in_ap), reads, writes, chan=chan, name=name)

    def mm(out_ap, lhsT, rhs, start, stop, reads, writes, name="mm"):
        return S.op("pe", lambda e: e.matmul(out_ap, lhsT, rhs, start=start, stop=stop), reads, writes, name=name)

    def act(out_ap, in_ap, func, bias=None, scale=None, reads=(), writes=(), name="act"):
        kw = {}
        if bias is not None:
            kw["bias"] = bias
        if scale is not None:
            kw["scale"] = scale
        return S.op("act", lambda e: e.activation(out_ap, in_ap, func, **kw), reads, writes, name=name)

    def tt(eng, out_ap, a, b, op, reads=(), writes=(), name="tt"):
        return S.op(eng, lambda e: e.tensor_tensor(out_ap, a, b, op), reads, writes, name=name)

    def ts(eng, out_ap, a, s1, s2, op0, op1=None, reads=(), writes=(), name="ts"):
        if op1 is None:
            return S.op(eng, lambda e: e.tensor_scalar(out_ap, a, s1, None, op0), reads, writes, name=name)
        return S.op(eng, lambda e: e.tensor_scalar(out_ap, a, s1, s2, op0, op1), reads, writes, name=name)

    def stt_op(eng, out_ap, a, scalar, b, op0, op1, reads=(), writes=(), name="stt"):
        return S.op(eng, lambda e: e.scalar_tensor_tensor(out_ap, a, scalar, b, op0, op1), reads, writes, name=name)

    def cp(eng, out_ap, in_ap, reads=(), writes=(), name="cp"):
        return S.op(eng, lambda e: e.tensor_copy(out_ap, in_ap), reads, writes, name=name)

    def mset(eng, ap, val, writes=(), name="memset"):
        return S.op(eng, lambda e: e.memset(ap, val), (), writes, name=name)

    def dbg_dump(key, ap, reads):
        if key in dbg:
            dma("sp", dbg[key], ap, "dbg_" + key, reads=reads, name="dbg")

    def finish():
        stats = S.emit(nc)
        return nc, stats

    dma("pool", ident, ident_d, "c_ident", writes=["ident"])
    dma("pool", perm, perm_d, "c_perm", writes=["perm"])
    dma("sp", cs_f, cvec, "c_cvec", writes=["cs_f"])
    dma("sp", b_ada_sb, b_ada_t, "c_bada", writes=["b_ada_sb"])
    dma("sp", b_in_sb, b_in_t, "c_bin", writes=["b_in_sb"])
    dma("sp", pscale, pscale_t, "c_psc", writes=["pscale"])
    dma("sp", invc, invc_d, "c_invc", writes=["invc"])
    dma("sp", umask, umask_d, "c_umask", writes=["umask"])
    mset("dve", ones_bf, 1.0, writes=["ones_bf"])
    mset("dve", selB[0:1, 0:64], 0.0, writes=["selB"])
    mset("dve", selB[0:1, 64:128], 1.0, writes=["selB"])
    ts("dve", bq8, b_in_sb[:, 16:24], 0.125, None, ALU.mult, reads=["b_in_sb"], writes=["bq8"])
    act(cs_f, cs_f, AF.Silu, reads=["cs_f"], writes=["cs_f"], name="silu_c")
    cp("dve", cs_b, cs_f, reads=["cs_f"], writes=["cs_b"])

    ph0 = cur[0]
    wada = tile([8, 3 * D], BF16)
    csrep = tile([8, 128], BF16)
    gtmp = tile([D], F32)
    w_ada_v = w_ada.rearrange("(kc p) n -> p kc n", p=128)
    for q in range(4):
        dma("pool", wada[:, 2 * q:2 * q + 2, 0:2 * D], w_ada_v[:, 2 * q:2 * q + 2, 0:2 * D], "wada%d" % q,
            writes=[("wada", q)])
    dma("sp", gtmp, b_ada_g.partition_broadcast(128), "c_gtmp", writes=["gtmp"])
    cs_b3 = cs_b.rearrange("p (k j) -> p k j", j=2)
    cs_f3 = cs_f.rearrange("p (k j) -> p k j", j=2)
    g_bc_off = None

    def sc1(kc, j):
        return ada3[:, 8 + kc, j:j + 1]

    def sh(kc, j):
        return ada3[:, kc, j:j + 1]

    xt = [tile([D], F32) for _ in range(4)]
    xn = [[tile([D], BF16) for _ in range(4)] for _ in range(2)]
    ada3 = ada.rearrange("p (o j) -> p o j", j=2)

    def emit_ada():
        for oc in range(16):
            for kc in range(8):
                mm(bk(4)[:, 2 * oc:2 * oc + 2], wada[:, kc, oc * 128:(oc + 1) * 128], cs_b3[:, kc, :],
                   kc == 0, kc == 7, reads=[("wada", kc // 2), "cs_b"], writes=[PB(4)], name="ada_mm")
        ps_ada = bk(4)[:, 0:32].rearrange("p (o j) -> p o j", j=2)
        for j in range(2):
            tt("dve", ada3[:, :, j], ps_ada[:, :, j], b_ada_sb[:, 0:16], ALU.add,
               reads=[PB(4), "b_ada_sb"], writes=["ada"])
        ts("dve", ada3[:, 8:16, :], ada3[:, 8:16, :], 1.0, None, ALU.add, reads=["ada"], writes=["ada"])
        for kc in range(8):
            ts("dve", csrep[:, kc, :], ones_bf, cs_f3[:, kc, 0:1], None, ALU.mult,
               reads=["ones_bf", "cs_f"], writes=["csrep"])

    def emit_g():
        for q in range(4):
            dma("pool", wada[:, 2 * q:2 * q + 2, 2 * D:3 * D], w_ada_v[:, 2 * q:2 * q + 2, 2 * D:3 * D], "wadag%d" % q,
                writes=[("wadag", q)])
        for n in range(2):
            for kc in range(8):
                mm(bk(6 + n), csrep[:, kc, :], wada[:, kc, 2 * D + n * 512:2 * D + (n + 1) * 512],
                   kc == 0, kc == 7, reads=["csrep", ("wadag", kc // 2)], writes=[PB(6 + n)], name="g_mm")
            tt("dve", g_bc[:, n * 512:(n + 1) * 512], bk(6 + n), gtmp[:, n * 512:(n + 1) * 512], ALU.add,
               reads=[PB(6 + n), "gtmp"], writes=["g_bc"])


    if stop == "0":
        emit_ada()
        emit_g()
        dbg_dump("ada", ada, ["ada"])
        dbg_dump("g_bc", g_bc, ["g_bc"])
        return finish()

    tiles = [(xs[(g_ * 4 + t_) * 128:(g_ * 4 + t_ + 1) * 128, :], g_, t_) for g_ in range(5) for t_ in range(4)]
    tiles += [(ctx[t_ * 128:(t_ + 1) * 128, :], 5, t_) for t_ in range(2)]

    def ln_part1(i):
        src_ap, grp, t4 = tiles[i]
        xb = i % 4
        sb = i % 3
        dma("sp", xt[xb], src_ap, "xt%d" % xb, writes=[("xt", xb)], name="ld_x")
        for hf in range(2):
            S.op("dve", lambda e, hf=hf: e.bn_stats(stt[:, sb * 12 + hf * 6: sb * 12 + hf * 6 + 6],
                                                    xt[xb][:, hf * 512:(hf + 1) * 512]),
                 reads=[("xt", xb)], writes=[("stt", sb)], name="bn_stats")
        S.op("dve", lambda e: e.bn_aggr(mv[:, sb * 2: sb * 2 + 2], stt[:, sb * 12: sb * 12 + 12]),
             reads=[("stt", sb)], writes=[("mv", sb)], name="bn_aggr")
        act(rs[:, sb:sb + 1], mv[:, sb * 2 + 1: sb * 2 + 2], AF.Sqrt, bias=LN_EPS,
            reads=[("mv", sb)], writes=[("rs", sb)], name="ln_sqrt")

    def ln_part2(i):
        src_ap, grp, t4 = tiles[i]
        xb = i % 4
        sb = i % 3
        par = grp % 2
        ntile = 4 if grp < 5 else 2
        S.op("dve", lambda e: e.reciprocal(rs[:, sb:sb + 1], rs[:, sb:sb + 1]),
             reads=[("rs", sb)], writes=[("rs", sb)], name="ln_rstd")
        ts("dve", nmr[:, sb:sb + 1], mv[:, sb * 2: sb * 2 + 1], rs[:, sb:sb + 1], -1.0, ALU.mult, ALU.mult,
           reads=[("mv", sb), ("rs", sb)], writes=[("nmr", sb)])
        act(xn[par][t4], xt[xb], AF.Identity, bias=nmr[:, sb:sb + 1], scale=rs[:, sb:sb + 1],
            reads=[("xt", xb), ("rs", sb), ("nmr", sb)], writes=[("xn", par, t4)], name="ln_norm")
        for kc in range(8):
            b = par * 4 + kc // 2
            o_ap = bkbf(b)[:, (kc % 2) * 512 + t4 * 128:(kc % 2) * 512 + (t4 + 1) * 128]
            S.op("pe", lambda e, o_ap=o_ap, i_ap=xn[par][t4][:, kc * 128:(kc + 1) * 128]:
                 e.transpose(o_ap, i_ap, ident),
                 reads=[("xn", par, t4), "ident"], writes=[PB(b)], name="transpose")
        if t4 != ntile - 1:
            return
        if grp == 0:
            emit_ada()
        for kc in range(8):
            b = par * 4 + kc // 2
            src = bkbf(b)[:, (kc % 2) * 512:(kc % 2) * 512 + ntile * 128]
            j = 0 if grp < 5 else 1
            if grp < 5:
                dst = hxT[:, kc, grp * 512:(grp + 1) * 512]
                wr = ("hxT", grp, kc)
            else:
                dst = hcT[:, kc, :]
                wr = "hcT"
            if (kc // 2) % 2 == 0:
                ts("dve", dst, src, sc1(kc, j), sh(kc, j), ALU.mult, ALU.add,
                   reads=[PB(b), "ada"], writes=[wr], name="mod_evac")
            else:
                act(dst, src, AF.Identity, bias=sh(kc, j), scale=sc1(kc, j),
                    reads=[PB(b), "ada"], writes=[wr], name="mod_evac")
        if grp == 5:
            emit_g()

    NT = len(tiles)
    P0OFF = ARENA - 12288 - 8192 - 5632 - 64
    assert cur[0] <= P0OFF, ("opening phase overlaps pair-0 prefetch region", cur[0])
    Wp0 = view(P0OFF, [8, 512], BF16)
    Eb0 = view(P0OFF + 8192, [2, NTAB * 128], BF16)
    w_in_v0 = w_in.rearrange("(kc p) n -> p kc n", p=128)
    btab_v0 = btab.rearrange("c p (h n) -> c p h n", h=2)

    def prefetch_pair0():
        for j, base in enumerate((2 * D, 3 * D, 4 * D, 5 * D)):
            dma("pool", Wp0[:, :, j * 128:(j + 1) * 128], w_in_v0[:, :, base: base + 128], "wp0_%d" % j,
                writes=[("Wp", 0, j)], name="ld_wp")
        dma("pool", Eb0, btab_v0[0], "eb0", writes=[("Eb", 0)], name="ld_eb")

    ln_part1(0)
    ln_part1(1)
    for i in range(NT):
        if i + 2 < NT:
            ln_part1(i + 2)
        ln_part2(i)
        if i == 12:
            prefetch_pair0()
    dbg_dump("hxT", hxT, [("hxT", g_, k_) for g_ in range(5) for k_ in range(8)])
    dbg_dump("hcT", hcT, ["hcT"])
    if stop == "A":
        return finish()

    S.barrier()
    cur[0] = ph0

    phB = cur[0]
    ctab = [tile([512], F32) for _ in range(2)]
    stab = [tile([512], F32) for _ in range(2)]
    Wp = [Wp0, tile([8, 512], BF16)]
    Eb = [Eb0, tile([2, NTAB * 128], BF16)]
    kT = tile([NSLOT], BF16)
    krot = tile([NSLOT], BF16)
    qP = tile([2, NOWN], BF16)
    qrotP = tile([2, NOWN], BF16)
    sz = tile([NOWN], BF16)
    Vaug = tile([20, 193], BF16)
    Vctx = tile([2, 193], BF16)
    kctxT = tile([256], BF16)
    t1 = [tile([512], F32) for _ in range(2)]
    t2 = [tile([512], F32) for _ in range(2)]
    PTW = [tile([1024], BF16) for _ in range(5)]
    rdf = [tile([512], F32) for _ in range(2)]
    rbc = [tile([512], F32) for _ in range(2)]
    g2 = [tile([512], F32) for _ in range(2)]
    tq = [tile([512], F32) for _ in range(2)]

    TOPO = ARENA - 12288
    assert cur[0] <= P0OFF, ("phase B overlaps pair-0 prefetch region", cur[0])
    Wq0 = view(TOPO, [8, 512], BF16)
    wpl = view(TOPO + 8192, [8, 256], BF16)
    w_pool_v = w_pool.rearrange("(g k p) n -> p (g k) n", g=4, k=2)
    for Qt in (qP, qrotP):
        mset("pool", Qt[64:128, 0, :], 0.0, writes=["qpad"])
        mset("pool", Qt[0:64, 1, :], 0.0, writes=["qpad"])
    for Vt in (Vaug, Vctx):
        mset("dve", Vt[:, :, 64:66], 1.0, writes=["Vconst"])
        mset("dve", Vt[:, :, 66:129], 0.0, writes=["Vconst"])

    w_in_v = w_in.rearrange("(kc p) n -> p kc n", p=128)
    btab_v = btab.rearrange("c p (h n) -> c p h n", h=2)
    PROJ = [7, 0, 1, 2, 3, 4, 5]
    pj = [0]

    def proj_bank():
        b = PROJ[pj[0] % len(PROJ)]
        pj[0] += 1
        return b

    def load_pair(c):
        buf = c % 2
        for j, base in enumerate((2 * D, 3 * D, 4 * D, 5 * D)):
            dma("pool", Wp[buf][:, :, j * 128:(j + 1) * 128],
                w_in_v[:, :, base + c * 128: base + (c + 1) * 128], "wp%d_%d" % (buf, j),
                writes=[("Wp", buf, j)], name="ld_wp")
        dma("pool", Eb[buf], btab_v[c], "eb%d" % buf, writes=[("Eb", buf)], name="ld_eb")

    def exp_tab(c):
        buf = c % 2
        act(Eb[buf], Eb[buf], AF.Exp, reads=[("Eb", buf)], writes=[("Eb", buf)], name="exp_tab")

    ropei = [0]

    def rope_tabs(off):
        i = ropei[0]
        ropei[0] += 1
        dma("sp", ctab[i % 2], ropeC[:, off:off + 512], "ctab%d" % (i % 2), writes=[("ctab", i % 2)], name="ld_rope")
        dma("sp", stab[i % 2], ropeS[:, off:off + 512], "stab%d" % (i % 2), writes=[("stab", i % 2)], name="ld_rope")
        return i % 2

    nbatch = [0]
    carry = []
    for c in range(npairs):
        buf = c % 2
        W = Wp[buf]
        if c + 1 < npairs:
            load_pair(c + 1)
        else:
            dma("pool", wpl, w_pool_v, "wpl", writes=["wpl"])
            dma("pool", Wq0[:, :, 0:256], w_in_v[:, :, 0:256], "wq0_0", writes=[("Wq", 0, 0)])
            dma("pool", Wq0[:, :, 256:512], w_in_v[:, :, D:D + 256], "wq0_1", writes=[("Wq", 0, 1)])
        pend = []

        def flush():
            while pend:
                pend.pop(0)()

        def k_step(g):
            ti = rope_tabs(g * 512)
            b = proj_bank()
            for kc in range(8):
                mm(bk(b), W[:, kc, 128:256], hxT[:, kc, g * 512:(g + 1) * 512], kc == 0, kc == 7,
                   reads=[("Wp", buf, 1), ("hxT", g, kc)], writes=[PB(b)], name="k_mm")
            sl = slice(g * 512, (g + 1) * 512)
            act(kT[:, sl], bk(b), AF.Identity, bias=b_in_sb[:, 24 + c:25 + c],
                reads=[PB(b), "b_in_sb"], writes=[("kT", g)], name="k_evac")

            def later():
                rb = proj_bank()
                mm(bk(rb), perm, kT[:, sl], True, True, reads=["perm", ("kT", g)], writes=[PB(rb)], name="rope_perm")
                tt("dve", t2[ti], bk(rb), stab[ti], ALU.mult, reads=[PB(rb), ("stab", ti)], writes=[("t2", ti)])
                tt("dve", t1[ti], kT[:, sl], ctab[ti], ALU.mult, reads=[("kT", g), ("ctab", ti)],
                   writes=[("t1", ti, 0), ("t1", ti, 1)])
                tt("dve", krot[:, sl], t1[ti], t2[ti], ALU.add,
                   reads=[("t1", ti, 0), ("t1", ti, 1), ("t2", ti)], writes=[("krot", g)])
            return later

        def q_step(g):
            ti = rope_tabs(OWN0 + g * 512)
            b = proj_bank()
            osl = slice(OWN0 + g * 512, OWN0 + (g + 1) * 512)
            sl = slice(g * 512, (g + 1) * 512)
            for kc in range(8):
                mm(bk(b), W[:, kc, 0:128], hxT[:, kc, osl], kc == 0, kc == 7,
                   reads=[("Wp", buf, 0), ("hxT", g, kc), ("hxT", g + 1, kc)], writes=[PB(b)], name="q_mm")
            for hh in range(2):
                hp = slice(64 * hh, 64 * hh + 64)
                act(qP[hp, hh, sl], bk(b)[hp, :], AF.Identity, bias=bq8[hp, c:c + 1], scale=0.125,
                    reads=[PB(b), "bq8", "qpad"], writes=[("qP", g, hh)], name="q_evac")

            def later():
                rb = proj_bank()
                for hh in range(2):
                    mm(bk(rb), perm, qP[:, hh, sl], hh == 0, hh == 1, reads=["perm", ("qP", g, hh), "qpad"],
                       writes=[PB(rb)], name="rope_perm")
                tt("dve", t2[ti], bk(rb), stab[ti], ALU.mult, reads=[PB(rb), ("stab", ti)], writes=[("t2", ti)])
                for hh in range(2):
                    hp = slice(64 * hh, 64 * hh + 64)
                    tt("dve", t1[ti][hp, :], qP[hp, hh, sl], ctab[ti][hp, :], ALU.mult,
                       reads=[("qP", g, hh), ("ctab", ti)], writes=[("t1", ti, hh)])
                    tt("dve", qrotP[hp, hh, sl], t1[ti][hp, :], t2[ti][hp, :], ALU.add,
                       reads=[("t1", ti, hh), ("t2", ti), "qpad"], writes=[("qrotP", g, hh)])
            return later

        def z_step(g):
            b = proj_bank()
            osl = slice(OWN0 + g * 512, OWN0 + (g + 1) * 512)
            sl = slice(g * 512, (g + 1) * 512)
            for kc in range(8):
                mm(bk(b), W[:, kc, 384:512], hxT[:, kc, osl], kc == 0, kc == 7,
                   reads=[("Wp", buf, 3), ("hxT", g, kc), ("hxT", g + 1, kc)], writes=[PB(b)], name="za_mm")
            act(sz[:, sl], bk(b), AF.Silu, bias=b_in_sb[:, 40 + c:41 + c],
                reads=[PB(b), "b_in_sb"], writes=[("sz", g)], name="za_evac")
            return None

        def v_step(g):
            b = proj_bank()
            for t4 in range(4):
                t = g * 4 + t4
                for kc in range(8):
                    mm(bk(b)[:, t4 * 128:(t4 + 1) * 128], hxT[:, kc, t * 128:(t + 1) * 128], W[:, kc, 256:384],
                       kc == 0, kc == 7, reads=[("Wp", buf, 2), ("hxT", g, kc)], writes=[PB(b)], name="v_mm")
            for hh in range(2):
                src = bk(b).rearrange("p (t h d) -> p t h d", t=4, h=2)[:, :, hh, :]
                dst = Vaug[:, g * 4:(g + 1) * 4, hh * 129: hh * 129 + 64]
                if g % 2 == 0:
                    cp("dve", dst, src, reads=[PB(b)], writes=[("Vaug", g, hh)], name="v_evac")
                else:
                    S.op("act", lambda e, dst=dst, src=src: e.copy(dst, src), reads=[PB(b)],
                         writes=[("Vaug", g, hh)], name="v_evac")
            return None

        def ctx_step():
            b = proj_bank()
            for kc in range(8):
                mm(bk(b)[:, 0:256], W[:, kc, 128:256], hcT[:, kc, :], kc == 0, kc == 7,
                   reads=[("Wp", buf, 1), "hcT"], writes=[PB(b)], name="kc_mm")
            act(kctxT, bk(b)[:, 0:256], AF.Identity, bias=b_in_sb[:, 24 + c:25 + c],
                reads=[PB(b), "b_in_sb"], writes=["kctxT"], name="kc_evac")
            b = proj_bank()
            for m in range(2):
                for kc in range(8):
                    mm(bk(b)[:, m * 128:(m + 1) * 128], hcT[:, kc, m * 128:(m + 1) * 128], W[:, kc, 256:384],
                       kc == 0, kc == 7, reads=[("Wp", buf, 2), "hcT"], writes=[PB(b)], name="vc_mm")
            for hh in range(2):
                src = bk(b)[:, 0:256].rearrange("p (t h d) -> p t h d", t=2, h=2)[:, :, hh, :]
                dst = Vctx[:, :, hh * 129: hh * 129 + 64]
                cp("dve", dst, src, reads=[PB(b)], writes=["Vctx"], name="vc_evac")
            return None

        steps = [(k_step, g) for g in range(5)] + [(q_step, g) for g in range(4)] + \
                [(z_step, g) for g in range(4)] + [(v_step, g) for g in range(5)] + [(ctx_step, None)]
        for si_, (fnc, arg) in enumerate(steps):
            lat = fnc(arg) if arg is not None else fnc()
            flush()
            if lat is not None:
                pend.append(lat)
            if si_ == 6:
                while carry:
                    carry.pop(0)()
        flush()
        while carry:
            carry.pop(0)()

        WAVES = [
            [("ctx", 0, 0), ("ctx", 1, 512)],
            [("loc", 4, 0), ("loc", 3, 512)],
            [("loc", 5, 0), ("loc", 7, 384), ("loc", 2, 512), ("loc", 0, 896)],
            [("loc", 6, 0), ("loc", 1, 256)],
        ]
        MERGE = [[], [(0, 1)], [(0, 2), (1, 3)], [(0, 1)]]
        WIDTH = [1024, 1024, 1024, 512]
        items = [(hh, qb, wi) for hh in range(2) for qb in range(4) for wi in range(4)]

        def seg_info(kind, r, j0):
            if kind == "ctx":
                return 0, 3
            return max(0, r - 4), min(3, r)

        def emit_qk(n):
            hh, qb, wi = items[n]
            hp = slice(64 * hh, 64 * hh + 64)
            j0 = qb * 4
            slot = n % 3
            base = slot * 1024
            pt = PTW[n % 5]
            ptr = ("PTW", n % 5)
            for (kind, r, off) in WAVES[wi]:
                lo, hi = seg_info(kind, r, j0)
                nq_ = (hi - lo + 1) * 128
                bnk = (base + off) // 512
                o_ap = psall[:, base + off: base + off + nq_]
                qs = slice((j0 + lo) * 128, (j0 + hi + 1) * 128)
                if kind == "loc":
                    kt = j0 + r
                    mm(o_ap, krot[:, kt * 128:(kt + 1) * 128], qrotP[:, hh, qs], True, False,
                       reads=[("krot", kt // 4), ("qrotP", qb, hh), "qpad"], writes=[PB(bnk)], name="qk")
                    ids = [SPECIAL.get((j0 + s_, r - s_ - 2), 2 - (r - s_ - 2)) for s_ in range(lo, hi + 1)]
                    runs = []
                    st = 0
                    for i in range(1, len(ids) + 1):
                        if i == len(ids) or ids[i] != ids[i - 1] + 1:
                            runs.append((st, i))
                            st = i
                    for ri, (a0, a1) in enumerate(runs):
                        b_ap = Eb[buf][:, hh, ids[a0] * 128:(ids[a0] + a1 - a0) * 128]
                        mm(psall[:, base + off + a0 * 128: base + off + a1 * 128], ident, b_ap, False,
                           ri == len(runs) - 1, reads=["ident", ("Eb", buf)], writes=[PB(bnk)], name="bias_mm")
                else:
                    mm(o_ap, kctxT[:, r * 128:(r + 1) * 128], qP[:, hh, qs], True, True,
                       reads=["kctxT", ("qP", qb, hh), "qpad"], writes=[PB(bnk)], name="qkc")
            wd = WIDTH[wi]
            bset = [PB(slot * 2)] + ([PB(slot * 2 + 1)] if wd > 512 else [])
            act(pt[:, 0:wd], psall[:, base: base + wd], AF.Exp, reads=bset, writes=[ptr], name="exp")

        deferred = []

        def emit_pv(n):
            hh, qb, wi = items[n]
            hp = slice(64 * hh, 64 * hh + 64)
            vcol = slice(0, 65) if hh == 0 else slice(65, 193)
            xrows = slice(0, 65) if hh == 0 else slice(0, 128)
            j0 = qb * 4
            bi = nbatch[0]
            XB = 6 + (bi % 2)
            pt = PTW[n % 5]
            ptr = ("PTW", n % 5)
            segs = WAVES[wi]
            for si, (kind, r, off) in enumerate(segs):
                lo, hi = seg_info(kind, r, j0)
                nq_ = (hi - lo + 1) * 128
                if kind == "loc":
                    kt = j0 + r
                    lhsT = Vaug[:, kt, vcol]
                    rdv = ("Vaug", kt // 4, hh)
                else:
                    lhsT = Vctx[:, r, vcol]
                    rdv = "Vctx"
                mm(bk(XB)[xrows, lo * 128:(hi + 1) * 128], lhsT, pt[:, off:off + nq_],
                   wi == 0 and si == 0, wi == 3 and si == len(segs) - 1,
                   reads=[rdv, "Vconst", ptr], writes=[PB(XB)], name="pv")
            if wi != 3:
                return
            nbatch[0] += 1
            gi = bi % 2
            drow = slice(64, 65) if hh == 0 else slice(0, 1)
            act(rdf[gi][drow, :], bk(XB)[drow, :], AF.Ln, reads=[PB(XB)], writes=[("rdf", gi)], name="den_ln")
            cp("dve", tq[gi][hp, :], bk(XB)[hp, :], reads=[PB(XB)], writes=[("tq", gi)], name="o_copy")
            act(rdf[gi][drow, :], rdf[gi][drow, :], AF.Exp, scale=-1.0, reads=[("rdf", gi)], writes=[("rdf", gi)],
                name="den_rcp")
            dma("sp", rd_dram[bi:bi + 1, :], rdf[gi][drow, :], "rdw%d" % gi, reads=[("rdf", gi)],
                writes=[("rd_dram", bi)], name="rd_st")
            dma("sp", rbc[gi][hp, :], rd_dram[bi:bi + 1, :].partition_broadcast(64), "rdr%d" % gi,
                reads=[("rd_dram", bi)], writes=[("rbc", gi)], name="rd_ld")
            qsl = slice(qb * 512, (qb + 1) * 512)

            def later(cc=c):
                tt("dve", tq[gi][hp, :], tq[gi][hp, :], rbc[gi][hp, :], ALU.mult,
                   reads=[("tq", gi), ("rbc", gi)], writes=[("tq", gi)], name="tq")
                stt_op("dve", naT[hp, cc, qsl], tq[gi][hp, :], b_in_sb[hp, 32 + cc:33 + cc], sz[hp, qsl],
                       ALU.add, ALU.mult, reads=[("sz", qb), ("tq", gi), "b_in_sb"], writes=[("naT", cc)], name="na")
            deferred.append((n + 6, later))

        nq = 0
        NI = len(items)
        for n in range(NI):
            while nq < min(n + 3, NI):
                emit_qk(nq)
                nq += 1
            emit_pv(n)
            while deferred and deferred[0][0] <= n:
                deferred.pop(0)[1]()
        while deferred:
            carry.append(deferred.pop(0)[1])
        if c == npairs - 1:
            while carry:
                carry.pop(0)()

    dbg_dump("naT", naT, [("naT", c_) for c_ in range(8)])
    dbg_dump("krot", krot, [("krot", g_) for g_ in range(5)])
    dbg_dump("sz", sz, [("sz", g_) for g_ in range(4)])
    dbg_dump("Vaug", Vaug, [("Vaug", g_, h_) for g_ in range(5) for h_ in range(2)])
    if stop == "B":
        return finish()
    S.barrier()

    cur[0] = phB
    poolT = tile([8, NOWN], BF16)
    wout_lo = tile([8, D], BF16)
    after_poolT = cur[0]
    Wq = [Wq0, tile([8, 512], BF16)]
    ub_off = cur[0]
    ub = [tile([2080], F32) for _ in range(2)]
    wout_hi_pre = view(ub_off, [8, D], BF16)
    pa = tile([2080], F32)
    pb_ = tile([2080], F32)
    pooled = tile([2, NOWN], BF16)
    bt = tile([32], F32)

    def load_group(g):
        buf = g % 2
        dma("pool", Wq[buf][:, :, 0:256], w_in_v[:, :, 256 * g:256 * (g + 1)], "wq%d_0" % buf,
            writes=[("Wq", buf, 0)])
        dma("pool", Wq[buf][:, :, 256:512], w_in_v[:, :, D + 256 * g: D + 256 * (g + 1)], "wq%d_1" % buf,
            writes=[("Wq", buf, 1)])

    if npairs < 1:
        load_group(0)
    w_out_v = w_out.rearrange("(kc p) n -> p kc n", p=128)
    for q in range(2):
        dma("pool", wout_lo[:, 4 * q:4 * q + 4, :], w_out_v[:, 4 * q:4 * q + 4, :], "wout%d" % q, writes=[("wout", q)])
    PROJ[:] = [0, 1, 2, 3, 4, 5, 6, 7]
    szp8 = [tile([512], BF16) for _ in range(8)]
    assert cur[0] <= TOPO, ("pool phase overlaps top region", cur[0])
    for g in range(npool):
        buf = g % 2
        W = Wq[buf]
        if g + 1 < npool:
            load_group(g + 1)
        w = 2 ** (g + 1)
        N = 2080
        for e2 in range(2):
            pc = 2 * g + e2
            u = ub[e2]
            ur = ("ub", e2)
            bias = b_in_sb[:, pc:pc + 1]
            for g4 in range(4):
                b = proj_bank()
                for kc in range(8):
                    mm(bk(b), W[:, kc, e2 * 128:(e2 + 1) * 128], hxT[:, kc, OWN0 + g4 * 512: OWN0 + (g4 + 1) * 512],
                       kc == 0, kc == 7, reads=[("Wq", buf, 0), ("hxT", g4, kc), ("hxT", g4 + 1, kc)], writes=[PB(b)],
                       name="u_mm")
                act(u[:, 16 + g4 * 512: 16 + (g4 + 1) * 512], bk(b), AF.Identity, bias=bias,
                    reads=[PB(b), "b_in_sb"], writes=[(ur, g4)], name="u_evac")
            b = proj_bank()
            for side, s0 in enumerate((OWN0 - 16, OWN0 + NOWN)):
                for kc in range(8):
                    mm(bk(b)[:, side * 16:(side + 1) * 16], W[:, kc, e2 * 128:(e2 + 1) * 128], hxT[:, kc, s0:s0 + 16],
                       kc == 0, kc == 7, reads=[("Wq", buf, 0), ("hxT", 0, kc), ("hxT", 4, kc)], writes=[PB(b)],
                       name="uh_mm")
            for side, c0 in enumerate((0, 16 + NOWN)):
                act(u[:, c0:c0 + 16], bk(b)[:, side * 16:(side + 1) * 16], AF.Identity, bias=bias,
                    reads=[PB(b), "b_in_sb"], writes=[(ur, 4 + side)], name="uh_evac")
                tt("dve", u[:, c0:c0 + 16], u[:, c0:c0 + 16], umask[:, side * 16:(side + 1) * 16], ALU.mult,
                   reads=[(ur, 4 + side), "umask"], writes=[(ur, 4 + side)], name="uh_mask")
        for oc in range(2):
            pc = 2 * g + oc
            for g4 in range(4):
                zb = proj_bank()
                for kc in range(8):
                    mm(bk(zb), W[:, kc, 256 + oc * 128: 256 + (oc + 1) * 128],
                       hxT[:, kc, OWN0 + g4 * 512: OWN0 + (g4 + 1) * 512], kc == 0, kc == 7,
                       reads=[("Wq", buf, 1), ("hxT", g4, kc), ("hxT", g4 + 1, kc)], writes=[PB(zb)], name="zp_mm")
                act(szp8[oc * 4 + g4], bk(zb), AF.Silu, bias=b_in_sb[:, 8 + pc:9 + pc], reads=[PB(zb), "b_in_sb"],
                    writes=[("szp", oc * 4 + g4)], name="zp_evac")
        for e2 in range(2):
            u = ub[e2]
            ur = ("ub", e2)
            pa_, pan = pa, "pa"
            pb2, pbn = pb_, "pb"
            ue = [(ur, k) for k in range(6)]
            tt("dve", pa_[:, 1:N], u[:, 0:N - 1], u[:, 1:N], ALU.add, reads=ue, writes=[pan], name="s2")
            res, resn = pa_, pan
            if g >= 1:
                tt("dve", pb2[:, 2:N - 1], pa_[:, 1:N - 2], pa_[:, 3:N], ALU.add, reads=[pan], writes=[pbn], name="s4")
                res, resn = pb2, pbn
            if g >= 2:
                tt("dve", pa_[:, 4:N - 3], pb2[:, 2:N - 5], pb2[:, 6:N - 1], ALU.add, reads=[pbn], writes=[pan], name="s8")
                res, resn = pa_, pan
            if g >= 3:
                tt("dve", pb2[:, 8:N - 7], pa_[:, 4:N - 11], pa_[:, 12:N - 3], ALU.add, reads=[pan], writes=[pbn], name="s16")
                res, resn = pb2, pbn
            stt_op("dve", pooled[:, e2, :], res[:, 16:16 + NOWN], 1.0 / w, u[:, 16:16 + NOWN], ALU.mult, ALU.subtract,
                   reads=[resn] + ue, writes=[("pooled", e2)], name="pool_sub")
            for side, c0 in enumerate((0, NOWN - 16)):
                iv = invc[:, (g * 2 + side) * 16:(g * 2 + side + 1) * 16]
                tt("dve", bt[:, e2 * 16:(e2 + 1) * 16], res[:, 16 + c0:32 + c0], iv, ALU.mult, reads=[resn, "invc"],
                   writes=[("bt", e2)], name="edge")
                tt("dve", pooled[:, e2, c0:c0 + 16], bt[:, e2 * 16:(e2 + 1) * 16], u[:, 16 + c0:32 + c0], ALU.subtract,
                   reads=[("bt", e2)] + ue, writes=[("pooled", e2)], name="edge")
        if g == npool - 1:
            ubres = [(("ub", e_), k_) for e_ in range(2) for k_ in range(6)]
            for q in range(2, 4):
                dma("pool", wout_hi_pre[:, 4 * q - 8:4 * q - 4, :], w_out_v[:, 4 * q:4 * q + 4, :], "wout%d" % q,
                    writes=[("wout", q)] + ubres, name="ld_wout_hi")
        for oc in range(2):
            pc = 2 * g + oc
            for g4 in range(4):
                lb = proj_bank()
                for k2 in range(2):
                    mm(bk(lb), wpl[:, g * 2 + k2, oc * 128:(oc + 1) * 128], pooled[:, k2, g4 * 512:(g4 + 1) * 512],
                       k2 == 0, k2 == 1, reads=["wpl", ("pooled", 0), ("pooled", 1)], writes=[PB(lb)], name="pl_mm")
                stt_op("dve", poolT[:, pc, g4 * 512:(g4 + 1) * 512], bk(lb), pscale[:, pc:pc + 1], szp8[oc * 4 + g4],
                       ALU.mult, ALU.mult, reads=[PB(lb), "pscale", ("szp", oc * 4 + g4)], writes=[("poolT", pc)],
                       name="pl_evac")

    dbg_dump("poolT", poolT, [("poolT", c_) for c_ in range(8)])
    if stop == "C1":
        return finish()
    S.barrier()

    cur[0] = ub_off
    wout_hi = tile([8, D], BF16)
    assert ub_off == after_poolT + 8192
    lng = tile([D], F32)
    lnb = tile([D], F32)
    bo_f = tile([D], F32)
    bo_b = tile([D], BF16)
    xr = [tile([D], F32) for _ in range(2)]
    ty = [tile([D], F32) for _ in range(2)]
    zz = [tile([D], F32) for _ in range(2)]
    ot = [tile([D], F32) for _ in range(2)]
    dma("sp", lng, ln_g.partition_broadcast(128), "c_lng", writes=["lng"])
    dma("sp", lnb, ln_b.partition_broadcast(128), "c_lnb", writes=["lnb"])
    dma("sp", bo_f[0:1, :], b_out, "c_bo", writes=["bo_f"])
    cp("dve", bo_b[0:1, :], bo_f[0:1, :], reads=["bo_f"], writes=["bo_b"])

    xr3 = xr + [tile([D], F32)]

    def c2_load(t):
        dma("sp", xr3[t % 3], xs[OWN0 + t * 128: OWN0 + (t + 1) * 128, :], "xr%d" % (t % 3), writes=[("xr", t % 3)])

    zz3 = [view(hx_off + i_ * 4096, [D], F32) for i_ in range(3)]
    ty3 = [view(hx_off + 12288 + i_ * 4096, [D], F32) for i_ in range(3)]

    def c2_a(t):
        par = t % 2
        p3 = t % 3
        sb = t % 3
        tsl = slice(t * 128, (t + 1) * 128)
        for nh in range(2):
            b = par * 2 + nh
            for kc in range(16):
                lhsT = poolT[:, kc, tsl] if kc < 8 else naT[:, kc - 8, tsl]
                rdr = ("poolT", kc) if kc < 8 else ("naT", kc - 8)
                wsl = wout_lo[:, kc, nh * 512:(nh + 1) * 512] if kc < 8 else wout_hi[:, kc - 8, nh * 512:(nh + 1) * 512]
                mm(bk(b), lhsT, wsl, kc == 0, False,
                   reads=[rdr, ("wout", kc // 4)], writes=[PB(b)], name="out_mm")
            mm(bk(b), ones_bf[0:1, :], bo_b[0:1, nh * 512:(nh + 1) * 512], False, True,
               reads=["ones_bf", "bo_b"], writes=[PB(b)], name="out_bias")
        tt("dve", ty3[p3], psall[:, par * 1024:(par + 1) * 1024], g_bc, ALU.mult,
           reads=[PB(par * 2), PB(par * 2 + 1), "g_bc"], writes=[("ty3", p3)], name="gate_y")
        stt_op("dve", zz3[p3], xr3[t % 3], ALPHA, ty3[p3], ALU.mult, ALU.add,
               reads=[("xr", t % 3), ("ty3", p3)], writes=[("zz3", p3)], name="resid")
        for hf in range(2):
            S.op("dve", lambda e, hf=hf: e.bn_stats(
                stt[:, sb * 12 + hf * 6: sb * 12 + hf * 6 + 6], zz3[p3][:, hf * 512:(hf + 1) * 512]),
                reads=[("zz3", p3)], writes=[("stt", sb)], name="bn_stats")
        S.op("dve", lambda e: e.bn_aggr(mv[:, sb * 2: sb * 2 + 2], stt[:, sb * 12: sb * 12 + 12]),
             reads=[("stt", sb)], writes=[("mv", sb)], name="bn_aggr")
        act(rs[:, sb:sb + 1], mv[:, sb * 2 + 1: sb * 2 + 2], AF.Sqrt, bias=LN_EPS,
            reads=[("mv", sb)], writes=[("rs", sb)], name="ln_sqrt")

    def c2_b1(t):
        p3 = t % 3
        sb = t % 3
        S.op("dve", lambda e: e.reciprocal(rs[:, sb:sb + 1], rs[:, sb:sb + 1]),
             reads=[("rs", sb)], writes=[("rs", sb)], name="ln_rstd")
        ts("dve", nmr[:, sb:sb + 1], mv[:, sb * 2: sb * 2 + 1], rs[:, sb:sb + 1], -1.0, ALU.mult, ALU.mult,
           reads=[("mv", sb), ("rs", sb)], writes=[("nmr", sb)])
        act(zz3[p3], zz3[p3], AF.Identity, bias=nmr[:, sb:sb + 1], scale=rs[:, sb:sb + 1],
            reads=[("zz3", p3), ("rs", sb), ("nmr", sb)], writes=[("zz3", p3)], name="ln2")

    def c2_b2(t):
        par = t % 2
        p3 = t % 3
        tsl = slice(t * 128, (t + 1) * 128)
        tt("dve", ty3[p3], zz3[p3], lng, ALU.mult, reads=[("zz3", p3), "lng"], writes=[("ty3", p3)], name="ln_g")
        tt("dve", ot[par], ty3[p3], lnb, ALU.add, reads=[("ty3", p3), "lnb"], writes=[("ot", par)], name="ln_b")
        dma("sp", out[tsl, :], ot[par], "out%d" % par, reads=[("ot", par)], name="store")

    c2_load(0)
    c2_load(1)
    c2_a(0)
    for t in range(16):
        if t + 2 < 16:
            c2_load(t + 2)
        if t + 1 < 16:
            c2_a(t + 1)
        c2_b1(t)
        if t >= 1:
            c2_b2(t - 1)
    c2_b2(15)

    return finish()


_CACHE = {}


def _prepare_inputs(x, c, ctx, c_ctx, w_ada, b_ada, w_in, b_in, w_pool, pool_scale, rpb,
                    w_out, b_out, ln_g, ln_b):
    f = lambda a: np.ascontiguousarray(np.asarray(a, dtype=np.float32))
    x, c, ctx, c_ctx = f(x), f(c), f(ctx), f(c_ctx)
    w_ada, b_ada, w_in, b_in = f(w_ada)[0], f(b_ada)[0], f(w_in)[0], f(b_in)[0]
    w_pool, pool_scale, rpb = f(w_pool)[0], f(pool_scale)[0], f(rpb)[0]
    w_out, b_out, ln_g, ln_b = f(w_out)[0], f(b_out)[0], f(ln_g)[0], f(ln_b)[0]
    shared = {
        "w_ada": w_ada, "b_ada_t": _tvec(b_ada, 24), "b_ada_g": b_ada[2 * D:3 * D].reshape(1, D).copy(),
        "w_in": w_in, "b_in_t": _tvec(b_in, 48),
        "w_pool": np.ascontiguousarray(w_pool.reshape(4 * 256, 256)), "pscale_t": _tvec(pool_scale, 8),
        "w_out": w_out, "b_out": b_out.reshape(1, D).copy(), "ln_g": ln_g.reshape(1, D).copy(),
        "ln_b": ln_b.reshape(1, D).copy(),
        "perm": _perm_matrix(), "ident": np.eye(128, dtype=np.float32),
    }
    in_maps = []
    for i in range(NCORES):
        b = i // 4
        r0 = (i % 4) * 32
        srow = _slot_rows(r0)
        xg = x[b].reshape(128, GW, D)[srow].reshape(NSLOT, D)
        cv = np.empty((128, 8, 2), np.float32)
        cv[:, :, 0] = c[b].reshape(8, 128).T
        cv[:, :, 1] = c_ctx.reshape(8, 128).T
        C, Sg = _rope_tables(r0)
        invc, um = _pool_tables(r0)
        m = dict(shared)
        m.update({
            "xs": np.ascontiguousarray(xg), "ctx": np.ascontiguousarray(ctx[b]),
            "cvec": np.ascontiguousarray(cv.reshape(128, 16)),
            "ropeC": C, "ropeS": Sg,
            "btab": np.ascontiguousarray(_bias_tables(rpb, r0).reshape(8, 128, 2 * NTAB * 128)),
            "invc": np.ascontiguousarray(invc.reshape(128, 128)),
            "umask": np.ascontiguousarray(um.reshape(128, 32)),
        })
        in_maps.append(m)
    return in_maps


def kernel(x, c, ctx, c_ctx, w_ada, b_ada, w_in, b_in, w_pool, pool_scale, rpb,
           w_out, b_out, ln_g, ln_b):
    in_maps = _prepare_inputs(x, c, ctx, c_ctx, w_ada, b_ada, w_in, b_in, w_pool, pool_scale, rpb,
                              w_out, b_out, ln_g, ln_b)
    if "nc" not in _CACHE:
        _CACHE["nc"] = build_program()[0]
    nc = _CACHE["nc"]
    res = run_bass_kernel_spmd(nc, in_maps, core_ids=list(range(NCORES)))
    outs = [np.asarray(r["out"], dtype=np.float32).reshape(NOWN, D) for r in res.results]
    full = np.stack([np.concatenate(outs[0:4], axis=0), np.concatenate(outs[4:8], axis=0)], axis=0)
    return full.astype(np.float32)
```
